# Optimizing a Trainium2 kernel written in Bass

```python
import math
import jax, jax.numpy as jnp
from jax import lax
import numpy as np

D_MODEL = 4096
BATCH = 16
SEQ = 256
DEPTH = 2
DEC_BATCH = 8
DEC_SEQ = 4096
PAST_LEN = 512

GRID_W = 64
HEAD_DIM = 128
H_A = D_MODEL // 4 // HEAD_DIM
MIX_A = H_A * HEAD_DIM
NA_ROWS = 8
NA_COLS = 16
NA_QCOLS = 16
NA_KCOLS = 2 * NA_COLS
H_B = D_MODEL // 4 // HEAD_DIM
DB_HALF = HEAD_DIM // 2
MIX_B = H_B * HEAD_DIM
H_C = D_MODEL // 2 // HEAD_DIM
KV_C = H_C // 4
G_C = H_C // KV_C
MIX_C = H_C * HEAD_DIM
KVW_C = KV_C * HEAD_DIM
WINDOW_C = 128
N_BRANCH = 3
PROJ_WIDTHS = (MIX_A, MIX_A, MIX_A, MIX_B, MIX_B, MIX_B, MIX_C, KVW_C, KVW_C, N_BRANCH * D_MODEL)
IN_W = 3 * MIX_A + 3 * MIX_B + MIX_C + 2 * KVW_C + N_BRANCH * D_MODEL
MIX_W = MIX_A + MIX_B + MIX_C
D_FF = ((8 * D_MODEL // 3 + 255) // 256) * 256
Q_BLOCK = 128
ROPE_BASE = 10000.0
LN_EPS = 1e-5
RMS_EPS = 1e-5
ALPHA = (2 * DEPTH) ** 0.25
BETA = (8 * DEPTH) ** -0.25
NEG_INF = -1e30

kernel_name = 'hybrid_diffusion_trunk_ctx_and_denoise_step'


def layer_norm(x, g, b):
    xf = x.astype(jnp.float32)
    mu = jnp.mean(xf, axis=-1, keepdims=True)
    var = jnp.mean(jnp.square(xf - mu), axis=-1, keepdims=True)
    return ((xf - mu) * lax.rsqrt(var + LN_EPS) * g.astype(jnp.float32) + b.astype(jnp.float32)).astype(x.dtype)


def ada_modulation(cvec, w_ada_l, b_ada_l):
    m = jax.nn.silu(cvec) @ w_ada_l + b_ada_l
    return m.reshape(cvec.shape[0], 1, 6, D_MODEL)


def modulate(x, mod, i):
    return x * (1 + mod[..., 3 * i + 1, :]) + mod[..., 3 * i, :]


def post_norm_residual(x, y, mod, i, g, b):
    return layer_norm(ALPHA * x + mod[..., 3 * i + 2, :] * y, g, b)


def rope_2d(x):
    t, d = x.shape[1], x.shape[-1]
    half, quarter = d // 2, d // 4
    pos = jnp.arange(t)
    row = (pos // GRID_W).astype(jnp.float32)
    col = (pos % GRID_W).astype(jnp.float32)
    inv = jnp.exp(-math.log(ROPE_BASE) * jnp.arange(quarter, dtype=jnp.float32) / quarter)
    bshape = (1, t) + (1,) * (x.ndim - 3) + (quarter,)

    def rot(xp, p):
        ang = (p[:, None] * inv[None, :]).reshape(bshape)
        cos, sin = jnp.cos(ang), jnp.sin(ang)
        x1, x2 = xp[..., :quarter], xp[..., quarter:]
        return jnp.concatenate([x1 * cos - x2 * sin, x2 * cos + x1 * sin], axis=-1)

    xf = x.astype(jnp.float32)
    return jnp.concatenate([rot(xf[..., :half], row), rot(xf[..., half:], col)], axis=-1).astype(x.dtype)


def map_query_blocks(fn, q):
    b, t = q.shape[:2]
    qb = jnp.moveaxis(q.reshape((b, t // Q_BLOCK, Q_BLOCK) + q.shape[2:]), 1, 0)
    out = jnp.moveaxis(lax.map(fn, qb), 0, 1)
    return out.reshape((b, t) + out.shape[3:])


def project(h, w_in_l):
    b, t = h.shape[:2]
    z = h @ w_in_l
    bounds = np.cumsum(PROJ_WIDTHS)[:-1].tolist()
    qa, ka, va, qb, kb, vb, qc, kc, vc, g = jnp.split(z, bounds, axis=-1)
    return (qa.reshape(b, t, H_A, HEAD_DIM), ka.reshape(b, t, H_A, HEAD_DIM), va.reshape(b, t, H_A, HEAD_DIM),
            qb.reshape(b, t, H_B, 2, DB_HALF), kb.reshape(b, t, H_B, 2, DB_HALF), vb.reshape(b, t, H_B, HEAD_DIM),
            qc.reshape(b, t, KV_C, G_C, HEAD_DIM), kc.reshape(b, t, KV_C, HEAD_DIM), vc.reshape(b, t, KV_C, HEAD_DIM),
            g.reshape(b, t, N_BRANCH, D_MODEL))


def dense_attn(q, k, v):
    s = jnp.einsum('bqhd,bkhd->bhqk', q, k, preferred_element_type=jnp.float32) * (q.shape[-1] ** -0.5)
    p = jax.nn.softmax(s, axis=-1).astype(v.dtype)
    return jnp.einsum('bhqk,bkhd->bqhd', p, v)


def diff_lambda(lam_l, layer):
    lam_init = 0.8 - 0.6 * math.exp(-0.3 * layer)
    lf = lam_l.astype(jnp.float32)
    lam = jnp.exp(jnp.sum(lf[0] * lf[1])) - jnp.exp(jnp.sum(lf[2] * lf[3])) + lam_init
    return lam, lam_init


def diff_core(q, k, v, lam):
    s = jnp.einsum('bqhid,bkhid->ibhqk', q, k, preferred_element_type=jnp.float32) * (DB_HALF ** -0.5)
    p = jax.nn.softmax(s, axis=-1)
    w = (p[0] - lam * p[1]).astype(v.dtype)
    return jnp.einsum('bhqk,bkhd->bqhd', w, v)


def diff_finish(o, gain, lam_init):
    of = o.astype(jnp.float32)
    of = of * lax.rsqrt(jnp.mean(jnp.square(of), axis=-1, keepdims=True) + RMS_EPS)
    return (of * gain.astype(jnp.float32) * (1.0 - lam_init)).astype(o.dtype)


def sink_column(sink, s):
    return jnp.broadcast_to(sink.astype(jnp.float32)[None, :, :, None, None], s.shape[:-1] + (1,))


def gqa_sink_dense(q, k, v, sink):
    s = jnp.einsum('bqngd,bknd->bngqk', q, k, preferred_element_type=jnp.float32) * (HEAD_DIM ** -0.5)
    p = jax.nn.softmax(jnp.concatenate([s, sink_column(sink, s)], axis=-1), axis=-1)
    return jnp.einsum('bngqk,bknd->bqngd', p[..., :-1].astype(v.dtype), v)


def window_gqa_latent(q, k, v, k_ctx, v_ctx, sink):
    b, t = q.shape[:2]
    nb = t // Q_BLOCK
    band = 3 * Q_BLOCK
    n_ctx = k_ctx.shape[1]
    pad = ((0, 0), (Q_BLOCK, Q_BLOCK), (0, 0), (0, 0))
    k_pad, v_pad = jnp.pad(k, pad), jnp.pad(v, pad)
    qoff = jnp.arange(Q_BLOCK)
    koff = jnp.arange(band)
    rel_ok = jnp.abs(qoff[:, None] + Q_BLOCK - koff[None, :]) <= WINDOW_C
    scale = HEAD_DIM ** -0.5

    def blk(args):
        i, qi = args
        kb = lax.dynamic_slice_in_dim(k_pad, i * Q_BLOCK, band, axis=1)
        vb = lax.dynamic_slice_in_dim(v_pad, i * Q_BLOCK, band, axis=1)
        kpos = (i - 1) * Q_BLOCK + koff
        valid = rel_ok & ((kpos >= 0) & (kpos < t))[None, :]
        s_loc = jnp.einsum('bqngd,bknd->bngqk', qi, kb, preferred_element_type=jnp.float32) * scale
        s_loc = jnp.where(valid, s_loc, NEG_INF)
        s_ctx = jnp.einsum('bqngd,bknd->bngqk', qi, k_ctx, preferred_element_type=jnp.float32) * scale
        p = jax.nn.softmax(jnp.concatenate([s_loc, s_ctx, sink_column(sink, s_loc)], axis=-1), axis=-1).astype(v.dtype)
        return (jnp.einsum('bngqk,bknd->bqngd', p[..., :band], vb)
                + jnp.einsum('bngqk,bknd->bqngd', p[..., band:band + n_ctx], v_ctx))

    qb = jnp.moveaxis(q.reshape(b, nb, Q_BLOCK, KV_C, G_C, HEAD_DIM), 1, 0)
    out = lax.map(blk, (jnp.arange(nb), qb))
    return jnp.moveaxis(out, 0, 1).reshape(b, t, KV_C, G_C, HEAD_DIM)


def neighbourhood_attn_latent(q, k, v, k_ctx, v_ctx, rpb_l):
    b, t, h, d = q.shape
    rows = t // GRID_W
    kr = min(NA_ROWS, rows)
    ncb = GRID_W // NA_QCOLS
    n_loc = kr * NA_KCOLS
    col_start = np.clip(np.arange(ncb) * NA_QCOLS - NA_COLS // 2, 0, GRID_W - NA_KCOLS)
    col_idx = col_start[:, None] + np.arange(NA_KCOLS)
    qcol = np.arange(ncb)[:, None] * NA_QCOLS + np.arange(NA_QCOLS)
    win_start = np.clip(qcol - NA_COLS // 2, 0, GRID_W - NA_COLS)
    kc = col_idx[:, None, :]
    col_valid = (kc >= win_start[..., None]) & (kc < win_start[..., None] + NA_COLS)
    dc_idx = np.clip(kc - qcol[..., None] + NA_COLS - 1, 0, 2 * NA_COLS - 2)
    scale = d ** -0.5
    kg = k.reshape(b, rows, GRID_W, h, d)
    vg = v.reshape(b, rows, GRID_W, h, d)

    def row_block(args):
        r, qr = args
        rs = jnp.clip(r - kr // 2, 0, rows - kr)
        k_blk = lax.dynamic_slice_in_dim(kg, rs, kr, axis=1)[:, :, col_idx]
        v_blk = lax.dynamic_slice_in_dim(vg, rs, kr, axis=1)[:, :, col_idx]
        s_loc = jnp.einsum('bjqhd,brjchd->bhjqrc', qr, k_blk, preferred_element_type=jnp.float32) * scale
        dr_idx = rs + jnp.arange(kr) - r + NA_ROWS - 1
        bias = rpb_l[:, dr_idx][:, :, dc_idx].transpose(0, 2, 3, 1, 4).astype(jnp.float32)
        s_loc = jnp.where(col_valid[:, :, None, :], s_loc + bias, NEG_INF).reshape(b, h, ncb, NA_QCOLS, n_loc)
        s_ctx = jnp.einsum('bjqhd,bchd->bhjqc', qr, k_ctx, preferred_element_type=jnp.float32) * scale
        p = jax.nn.softmax(jnp.concatenate([s_loc, s_ctx], axis=-1), axis=-1).astype(v.dtype)
        p_loc = p[..., :n_loc].reshape(b, h, ncb, NA_QCOLS, kr, NA_KCOLS)
        out = (jnp.einsum('bhjqrc,brjchd->bjqhd', p_loc, v_blk)
               + jnp.einsum('bhjqc,bchd->bjqhd', p[..., n_loc:], v_ctx))
        return out.reshape(b, GRID_W, h, d)

    qg = jnp.moveaxis(q.reshape(b, rows, ncb, NA_QCOLS, h, d), 1, 0)
    out = lax.map(row_block, (jnp.arange(rows), qg))
    return jnp.moveaxis(out, 0, 1).reshape(b, t, h, d)


def merge_branches(o_a, o_b, o_c, g, w_branch_l, w_out_l):
    b, t = o_a.shape[:2]
    gate = jax.nn.sigmoid(g.astype(jnp.float32)).astype(o_a.dtype)
    y_a = o_a.reshape(b, t, MIX_A) @ w_branch_l[:MIX_A]
    y_b = o_b.reshape(b, t, MIX_B) @ w_branch_l[MIX_A:MIX_A + MIX_B]
    y_c = o_c.reshape(b, t, MIX_C) @ w_branch_l[MIX_A + MIX_B:]
    return (gate[..., 0, :] * y_a + gate[..., 1, :] * y_b + gate[..., 2, :] * y_c) @ w_out_l


def context_mixer(h, w_in_l, lam, lam_init, subln_l, sink, w_branch_l, w_out_l):
    qa, ka, va, qb, kb, vb, qc, kc, vc, g = project(h, w_in_l)
    o_a = map_query_blocks(lambda qq: dense_attn(qq, ka, va), qa)
    o_b = diff_finish(map_query_blocks(lambda qq: diff_core(qq, kb, vb, lam), qb), subln_l, lam_init)
    o_c = map_query_blocks(lambda qq: gqa_sink_dense(qq, kc, vc, sink), qc)
    return merge_branches(o_a, o_b, o_c, g, w_branch_l, w_out_l), (ka, va, kb, vb, kc, vc)


def latent_mixer(h, cka, cva, ckb, cvb, ckc, cvc, w_in_l, rpb_l, lam, lam_init, subln_l, sink, w_branch_l, w_out_l):
    qa, ka, va, qb, kb, vb, qc, kc, vc, g = project(h, w_in_l)
    o_a = neighbourhood_attn_latent(qa, ka, va, cka, cva, rpb_l)
    kb_all = jnp.concatenate([rope_2d(kb), ckb], axis=1)
    vb_all = jnp.concatenate([vb, cvb], axis=1)
    o_b = diff_finish(map_query_blocks(lambda qq: diff_core(qq, kb_all, vb_all, lam), rope_2d(qb)), subln_l, lam_init)
    o_c = window_gqa_latent(rope_2d(qc), rope_2d(kc), vc, ckc, cvc, sink)
    return merge_branches(o_a, o_b, o_c, g, w_branch_l, w_out_l)


def swiglu(h, w_gate_up_l, w_down_l):
    gu = h @ w_gate_up_l
    return (jax.nn.silu(gu[..., :D_FF]) * gu[..., D_FF:]) @ w_down_l


def setup_inputs(seed: int = 0) -> dict:
    key = jax.random.key(seed)
    ks = jax.random.split(key, 26)

    def nrm(k, shape, s=1.0):
        return s * jax.random.normal(k, shape, jnp.float32)

    w_branch = jnp.concatenate([
        nrm(ks[15], (DEPTH, MIX_A, D_MODEL), MIX_A ** -0.5),
        nrm(ks[16], (DEPTH, MIX_B, D_MODEL), MIX_B ** -0.5),
        nrm(ks[17], (DEPTH, MIX_C, D_MODEL), MIX_C ** -0.5)], axis=1)
    return {
        'x_prompt': nrm(ks[0], (BATCH, SEQ, D_MODEL)),
        'x_sample': nrm(ks[1], (DEC_BATCH, DEC_SEQ, D_MODEL)),
        'cache_a_k': nrm(ks[2], (DEC_BATCH, DEPTH, PAST_LEN, H_A, HEAD_DIM)),
        'cache_a_v': nrm(ks[3], (DEC_BATCH, DEPTH, PAST_LEN, H_A, HEAD_DIM)),
        'cache_b_k': nrm(ks[4], (DEC_BATCH, DEPTH, PAST_LEN, H_B, 2, DB_HALF)),
        'cache_b_v': nrm(ks[5], (DEC_BATCH, DEPTH, PAST_LEN, H_B, HEAD_DIM)),
        'cache_c_k': nrm(ks[6], (DEC_BATCH, DEPTH, PAST_LEN, KV_C, HEAD_DIM)),
        'cache_c_v': nrm(ks[7], (DEC_BATCH, DEPTH, PAST_LEN, KV_C, HEAD_DIM)),
        'c': nrm(ks[8], (DEC_BATCH, D_MODEL)),
        'c_ctx': nrm(ks[9], (D_MODEL,)),
        'w_ada': nrm(ks[10], (DEPTH, D_MODEL, 6 * D_MODEL), 0.5 * D_MODEL ** -0.5),
        'b_ada': nrm(ks[11], (DEPTH, 6 * D_MODEL), 0.02),
        'w_in': nrm(ks[12], (DEPTH, D_MODEL, IN_W), D_MODEL ** -0.5),
        'rpb_a': nrm(ks[13], (DEPTH, H_A, 2 * NA_ROWS - 1, 2 * NA_COLS - 1), 0.5),
        'lam_b': nrm(ks[14], (DEPTH, 4, DB_HALF), 0.1),
        'subln_b': 1.0 + nrm(ks[18], (DEPTH, HEAD_DIM), 0.02),
        'sink_c': nrm(ks[19], (DEPTH, H_C)),
        'w_branch': w_branch,
        'w_out': nrm(ks[20], (DEPTH, D_MODEL, D_MODEL), BETA * D_MODEL ** -0.5),
        'ln_g': 1.0 + nrm(ks[21], (DEPTH, 2, D_MODEL), 0.02),
        'ln_b': nrm(ks[22], (DEPTH, 2, D_MODEL), 0.02),
        'w_gate_up': nrm(ks[23], (DEPTH, D_MODEL, 2 * D_FF), D_MODEL ** -0.5),
        'w_down': nrm(ks[24], (DEPTH, D_FF, D_MODEL), BETA * D_FF ** -0.5),
    }


def reference(x_prompt, x_sample, cache_a_k, cache_a_v, cache_b_k, cache_b_v, cache_c_k, cache_c_v, c, c_ctx,
              w_ada, b_ada, w_in, rpb_a, lam_b, subln_b, sink_c, w_branch, w_out, ln_g, ln_b, w_gate_up, w_down):
    xp, xs = x_prompt, x_sample
    new_lists = ([], [], [], [], [], [])
    for l in range(DEPTH):
        lam, lam_init = diff_lambda(lam_b[l], l)
        sink = sink_c[l].reshape(KV_C, G_C)
        mod_p = ada_modulation(c_ctx[None, :], w_ada[l], b_ada[l])
        y, ctx_kv = context_mixer(modulate(xp, mod_p, 0), w_in[l], lam, lam_init, subln_b[l], sink, w_branch[l], w_out[l])
        xp = post_norm_residual(xp, y, mod_p, 0, ln_g[l, 0], ln_b[l, 0])
        xp = post_norm_residual(xp, swiglu(modulate(xp, mod_p, 1), w_gate_up[l], w_down[l]), mod_p, 1, ln_g[l, 1], ln_b[l, 1])
        for lst, arr in zip(new_lists, ctx_kv):
            lst.append(arr)
        mod_s = ada_modulation(c, w_ada[l], b_ada[l])
        y = latent_mixer(modulate(xs, mod_s, 0), cache_a_k[:, l], cache_a_v[:, l], cache_b_k[:, l], cache_b_v[:, l],
                         cache_c_k[:, l], cache_c_v[:, l], w_in[l], rpb_a[l], lam, lam_init, subln_b[l], sink,
                         w_branch[l], w_out[l])
        xs = post_norm_residual(xs, y, mod_s, 0, ln_g[l, 0], ln_b[l, 0])
        xs = post_norm_residual(xs, swiglu(modulate(xs, mod_s, 1), w_gate_up[l], w_down[l]), mod_s, 1, ln_g[l, 1], ln_b[l, 1])
    new_a_k = jnp.stack(new_lists[0], axis=1)
    new_a_v = jnp.stack(new_lists[1], axis=1)
    new_b_k = jnp.stack(new_lists[2], axis=1)
    new_b_v = jnp.stack(new_lists[3], axis=1)
    new_c_k = jnp.stack(new_lists[4], axis=1)
    new_c_v = jnp.stack(new_lists[5], axis=1)
    return (xp, xs, new_a_k, new_a_v, new_b_k, new_b_v, new_c_k, new_c_v)
```

```python
import math
import numpy as np
from contextlib import ExitStack
import concourse.bass as bass
import concourse.mybir as mybir
from concourse.bass_utils import run_bass_kernel_spmd

F32 = mybir.dt.float32
BF16 = mybir.dt.bfloat16
AF = mybir.ActivationFunctionType
ALU = mybir.AluOpType
AX = mybir.AxisListType
NEG = -30000.0
LN_EPS = 1e-5
RMS_EPS = 1e-5
ROPE_BASE = 10000.0


class Cfg:
    def __init__(s, D=4096, S=4096, SEQ=256, NP=2, PAST=512, DEPTH=2):
        s.D = D; s.S = S; s.SEQ = SEQ; s.NP = NP; s.PAST = PAST; s.DEPTH = DEPTH
        s.HA = D // 4 // 128; s.HB = s.HA; s.HC = D // 2 // 128; s.KVC = s.HC // 4; s.GC = 4
        s.MIXA = s.HA * 128; s.MIXB = s.HB * 128; s.MIXC = s.HC * 128; s.KVW = s.KVC * 128
        s.INW = 3 * s.MIXA + 3 * s.MIXB + s.MIXC + 2 * s.KVW + 3 * D
        s.DFF = ((8 * D // 3 + 255) // 256) * 256
        s.KC = D // 128; s.FC = s.DFF // 128
        s.NTS = S // 512; s.T = S + NP * SEQ; s.NT = s.T // 512
        s.R = S // 64; s.NB = S // 128; s.TB = s.T // 128
        s.VW = s.MIXA + s.MIXB + s.KVW
        s.ALPHA = (2 * DEPTH) ** 0.25
        s.PB = PAST // 128
        assert NP * SEQ == 512 and s.FC % 2 == 0 and S % 512 == 0
        s.CQA = 0; s.CKA = s.HA; s.CQB = 2 * s.HA; s.CKB = 2 * s.HA + s.HB; s.CKC = 2 * s.HA + 2 * s.HB
        s.NQK = s.CKC + s.KVC


def na_variants(R):
    NB = R // 2
    c = np.arange(64)
    ws = np.clip(c - 8, 0, 48)

    def tile_idx(i, m):
        dr = np.zeros((128, 128), np.int64); dc = np.zeros((128, 128), np.int64); ok = np.zeros((128, 128), bool)
        for krl in range(2):
            for rl in range(2):
                kr = 2 * m + krl; r = 2 * i + rl
                rs = min(max(r - 4, 0), R - 8)
                rowok = (kr >= rs) and (kr < rs + 8)
                kc = c[:, None]; qc = c[None, :]
                colok = (kc >= ws[None, :]) & (kc < ws[None, :] + 16)
                sl = (slice(krl * 64, krl * 64 + 64), slice(rl * 64, rl * 64 + 64))
                ok[sl] = colok & rowok
                dr[sl] = np.clip(kr - r + 7, 0, 14)
                dc[sl] = np.clip(kc - qc + 15, 0, 30)
        return dr, dc, ok

    sigs = {}; blocks = []; var_tiles = []
    for i in range(NB):
        rs0 = min(max(2 * i - 4, 0), R - 8); rs1 = min(max(2 * i + 1 - 4, 0), R - 8)
        m0 = min(rs0, rs1) // 2; m1 = (max(rs0, rs1) + 7) // 2
        ms = list(range(m0, m1 + 1))
        tiles = [tile_idx(i, m) for m in ms]
        key = tuple((t[0].tobytes(), t[1].tobytes(), t[2].tobytes()) for t in tiles)
        if key not in sigs:
            sigs[key] = len(var_tiles)
            var_tiles.extend(tiles)
        blocks.append((ms, sigs[key]))
    return blocks, var_tiles


def rope_tables(S, kind):
    pos = np.arange(S)
    row = (pos // 64).astype(np.float64); col = (pos % 64).astype(np.float64)
    cos = np.zeros((128, S), np.float32); sin = np.zeros((128, S), np.float32)
    perm = np.zeros((128, 128), np.float32)
    d = 128 if kind == 'C' else 64
    half, quarter = d // 2, d // 4
    inv = np.exp(-math.log(ROPE_BASE) * np.arange(quarter, dtype=np.float32) / quarter).astype(np.float32)
    for i in range(128):
        w = i % d
        h = w // half; ww = w % half; first = ww < quarter; j = ww % quarter
        p = row if h == 0 else col
        ang = (p.astype(np.float32) * inv[j]).astype(np.float32)
        cos[i] = np.cos(ang); sn = np.sin(ang)
        sin[i] = -sn if first else sn
        partner = i + quarter if first else i - quarter
        perm[partner, i] = 1.0
    return cos, sin, perm


class Buf:
    __slots__ = ('w', 'r', 'x')

    def __init__(s, x=False):
        s.w = None; s.r = {}; s.x = x


class Sched:
    ENG = ('pe', 'act', 'dve', 'pool', 'sp')
    KQ = 16

    def __init__(s, nc, es):
        s.nc = nc
        s.eobj = {'pe': nc.tensor, 'act': nc.scalar, 'dve': nc.vector, 'pool': nc.gpsimd, 'sp': nc.sync}
        s.sem = {e: es.enter_context(nc.semaphore('c_' + e)) for e in s.ENG}
        s.cnt = {e: 0 for e in s.ENG}
        s.ring = {q: [es.enter_context(nc.semaphore('r_%s%d' % (q, i))) for i in range(s.KQ)] for q in ('sp', 'pool')}
        s.nd = {q: 0 for q in s.ring}
        s.ops = {e: [] for e in s.ENG}
        s.waited = {e: {} for e in s.ENG}
        s.semobj = {}
        s.bar = {}
        s.bar_pending = {e: False for e in s.ENG}
        s.nbuf = 0
        s.phase = 0
        s.dead = False
        import os as _os
        s.kstop = int(_os.environ.get('KSTOP', '0'))

    def sb(s, es, name, shape, dt):
        s.nbuf += 1
        t = es.enter_context(s.nc.sbuf_tensor('%s_%d' % (name, s.nbuf), list(shape), dt))
        return t, Buf()

    def ps(s, es, name, shape, dt=F32):
        s.nbuf += 1
        t = es.enter_context(s.nc.psum_tensor('%s_%d' % (name, s.nbuf), list(shape), dt))
        return t, Buf(True)

    def _tokwait(s, tok):
        if tok[0] == 'c':
            return s.sem[tok[1]], tok[2]
        return s.ring[tok[1]][tok[2]], tok[3]

    def _mk_waits(s, eng, toks):
        need = {}
        if s.bar_pending[eng]:
            s.bar_pending[eng] = False
            for k, (sem, v) in s.bar.items():
                need[k] = (sem, v)
        for tok in toks:
            if tok[0] == 'c' and tok[1] == 'pe' and eng == 'pe':
                continue
            sem, v = s._tokwait(tok)
            k = tok[:2] if tok[0] == 'c' else tok[:3]
            if k not in need or need[k][1] < v:
                need[k] = (sem, v)
        out = []
        wd = s.waited[eng]
        for k, (sem, v) in need.items():
            if wd.get(k, 0) >= v:
                continue
            wd[k] = v
            out.append((sem, v))
        return out

    def _deps(s, reads, writes, eng=None):
        toks = set()
        for b in reads:
            if b.w:
                toks.add(b.w)
            if b.x:
                for k, t in b.r.items():
                    if not (k[0] == 'c' and k[1] == eng):
                        toks.add(t)
        for b in writes:
            if b.w:
                toks.add(b.w)
            toks.update(b.r.values())
        return toks

    def _commit(s, tok, reads, writes):
        k = tok[:2] if tok[0] == 'c' else tok[:3]
        for b in reads:
            b.r[k] = tok
        for b in writes:
            b.w = tok; b.r = {}

    def op(s, eng, reads, writes, fn):
        if s.dead:
            return None
        toks = s._deps(reads, writes, eng)
        waits = s._mk_waits(eng, toks)
        s.cnt[eng] += 1
        tok = ('c', eng, s.cnt[eng])
        s.ops[eng].append((waits, fn, s.sem[eng], 1))
        s._commit(tok, reads, writes)
        return tok

    def dma(s, q, out, in_, reads=(), writes=(), **kw):
        if s.dead:
            return None
        toks = s._deps(reads, writes)
        j = s.nd[q]; slot = j % s.KQ; tgt = 16 * (j // s.KQ + 1)
        if j >= s.KQ:
            toks.add(('d', q, slot, tgt - 16))
        waits = s._mk_waits(q, toks)
        s.nd[q] += 1
        tok = ('d', q, slot, tgt)
        s.ops[q].append((waits, (lambda e: e.dma_start(out=out, in_=in_, **kw)), s.ring[q][slot], 16))
        s._commit(tok, reads, writes)
        return tok

    def barrier(s):
        s.bar = {}
        for e in s.ENG:
            if s.cnt[e]:
                s.bar[('c', e)] = (s.sem[e], s.cnt[e])
        for q in s.ring:
            for slot in range(s.KQ):
                n = (s.nd[q] - slot + s.KQ - 1) // s.KQ if s.nd[q] > slot else 0
                if n:
                    s.bar[('d', q, slot)] = (s.ring[q][slot], 16 * n)
        for e in s.ENG:
            s.bar_pending[e] = True

    def flush(s, final=False):
        if s.dead:
            return
        s.phase += 1
        print("phase", s.phase, {e: len(v) for e, v in s.ops.items()}, flush=True)
        if s.kstop and s.phase == s.kstop:
            final = True
            s.dead = True
        s.barrier()
        if final:
            for e in s.ENG:
                w = s._mk_waits(e, [])
                s.ops[e].append((w, None, None, 0))
        ops = s.ops
        s.ops = {e: [] for e in s.ENG}
        with s.nc.Block() as block:
            def mk(elist):
                def body(eng):
                    for waits, fn, sem, inc in elist:
                        for (ws, wv) in waits:
                            eng.wait_ge(ws, wv)
                        if fn is not None:
                            ins = fn(eng)
                            ins.then_inc(sem, inc)
                return body
            if ops['sp']:
                block.sync(mk(ops['sp']))
            if ops['pool']:
                block.gpsimd(mk(ops['pool']))
            if ops['act']:
                block.scalar(mk(ops['act']))
            if ops['dve']:
                block.vector(mk(ops['dve']))
            if ops['pe']:
                block.tensor(mk(ops['pe']))


class Ring:
    def __init__(s, items):
        s.items = items; s.i = 0

    def next(s):
        it = s.items[s.i % len(s.items)]; s.i += 1
        return it


def build(cfg):
    c = cfg
    D, S_, T, KC, FC, NT, NTS, DEPTH = c.D, c.S, c.T, c.KC, c.FC, c.NT, c.NTS, c.DEPTH
    nc = bass.Bass("TRN2", target_bir_lowering=False)

    def din(name, shape, dt=F32):
        return nc.dram_tensor(name, list(shape), dt, kind="ExternalInput").ap()

    def dout(name, shape):
        return nc.dram_tensor(name, list(shape), F32, kind="ExternalOutput").ap()

    def dscr(name, shape, dt):
        return nc.dram_tensor(name, list(shape), dt, kind="Internal").ap()

    xs = din("xs", [S_, D]); xp = din("xp", [512, D])
    cak = din("cak", [DEPTH, c.PAST, c.MIXA]); cav = din("cav", [DEPTH, c.PAST, c.MIXA])
    cbk = din("cbk", [DEPTH, c.PAST, c.MIXB]); cbv = din("cbv", [DEPTH, c.PAST, c.MIXB])
    cck = din("cck", [DEPTH, c.PAST, c.KVW]); ccv = din("ccv", [DEPTH, c.PAST, c.KVW])
    cvec = din("cvec", [2, D])
    w_ada = din("w_ada", [DEPTH, D, 6 * D]); b_ada = din("b_ada", [DEPTH, 6 * D])
    w_in = din("w_in", [DEPTH, D, c.INW]); w_branch = din("w_branch", [DEPTH, D, D])
    w_out = din("w_out", [DEPTH, D, D]); w_gu = din("w_gate_up", [DEPTH, D, 2 * c.DFF])
    w_down = din("w_down", [DEPTH, c.DFF, D])
    ln_g = din("ln_g", [DEPTH, 2, D]); ln_b = din("ln_b", [DEPTH, 2, D])
    lam_b = din("lam_b", [DEPTH, 256]); subln = din("subln_b", [DEPTH, 128]); sink = din("sink_c", [DEPTH, c.HC])
    blocks_na, var_tiles = na_variants(c.R)
    NV = len(var_tiles)
    rpbt = din("rpbt", [DEPTH, c.HA, NV, 128, 128])
    ident_d = din("ident", [128, 128])
    ropeB = din("ropeB", [3, 128, max(S_, 128)]); ropeC = din("ropeC", [3, 128, max(S_, 128)])
    trim = din("trim", [2, 128, 128])

    ys = dout("ys", [S_, D]); yp = dout("yp", [512, D])
    nak = dout("nak", [c.NP, DEPTH, c.SEQ, c.MIXA]); nav = dout("nav", [c.NP, DEPTH, c.SEQ, c.MIXA])
    nbk = dout("nbk", [c.NP, DEPTH, c.SEQ, c.MIXB]); nbv = dout("nbv", [c.NP, DEPTH, c.SEQ, c.MIXB])
    nck = dout("nck", [c.NP, DEPTH, c.SEQ, c.KVW]); ncv = dout("ncv", [c.NP, DEPTH, c.SEQ, c.KVW])

    mod_d = dscr("mod_d", [DEPTH, 2, 6 * D], F32)
    hT_d = dscr("hT_d", [NT, 128, KC, 512], BF16)
    QK_d = dscr("QK_d", [c.NQK, 128, T], BF16)
    QC_d = dscr("QC_d", [c.KVC, 128, c.TB, 4, 128], BF16)
    V_d = dscr("V_d", [c.TB, 128, c.VW], BF16)
    G_d = dscr("G_d", [NT, D // 512, 128, 3, 4, 512], BF16)
    O_d = dscr("O_d", [NT, 128, KC, 512], BF16)
    M_d = dscr("M_d", [NT, 128, KC, 512], BF16)
    A_d = dscr("A_d", [2 * NT, 128, FC, 256], BF16)
    y_d = dscr("y_d", [T, D], F32)
    x1_d = dscr("x1_d", [T, D], F32)
    x2_d = dscr("x2_d", [T, D], F32)

    def xrows(src, tb):
        if src == 'in':
            nsb = S_ // 128
            return xs[tb * 128:(tb + 1) * 128, :] if tb < nsb else xp[(tb - nsb) * 128:(tb - nsb + 1) * 128, :]
        if src == 'out':
            nsb = S_ // 128
            return ys[tb * 128:(tb + 1) * 128, :] if tb < nsb else yp[(tb - nsb) * 128:(tb - nsb + 1) * 128, :]
        return src[tb * 128:(tb + 1) * 128, :]

    with ExitStack() as top:
        S = Sched(nc, top)
        ident, ident_b = S.sb(top, 'ident', [128, 128], F32)
        identb, identb_b = S.sb(top, 'identb', [128, 128], BF16)
        ones_bf, ones_bf_b = S.sb(top, 'ones_bf', [128, 128], BF16)
        ones_f, ones_f_b = S.sb(top, 'ones_f', [128, 128], F32)
        eps_ln, eps_ln_b = S.sb(top, 'eps_ln', [128, 1], F32)
        eps_rms, eps_rms_b = S.sb(top, 'eps_rms', [128, 1], F32)
        modT = []
        for l in range(DEPTH):
            modT.append(S.sb(top, 'modT', [128, 6, KC, 2], F32))
        evac_flip = [0]

        def copy_evac(out_ap, in_ap, reads, writes):
            evac_flip[0] ^= 1
            import os as _os2
            _m = _os2.environ.get('KDBG', '')
            if (evac_flip[0] and 'dveonly' not in _m) or 'actonly' in _m:
                S.op('act', reads, writes, lambda e: e.activation(out=out_ap, in_=in_ap, func=AF.Copy))
            else:
                S.op('dve', reads, writes, lambda e: e.tensor_copy(out=out_ap, in_=in_ap))

        with ExitStack() as es:
            S.dma('sp', out=ident[:], in_=ident_d[:, :], writes=[ident_b])
            S.op('dve', [ident_b], [identb_b], lambda e: e.tensor_copy(out=identb[:], in_=ident[:]))
            S.op('dve', [], [ones_bf_b], lambda e: e.memset(ones_bf[:], 1.0))
            S.op('dve', [], [ones_f_b], lambda e: e.memset(ones_f[:], 1.0))
            S.op('dve', [], [eps_ln_b], lambda e: e.memset(eps_ln[:], LN_EPS))
            S.op('dve', [], [eps_rms_b], lambda e: e.memset(eps_rms[:], RMS_EPS))
            cT, cT_b = S.sb(es, 'cT', [128, KC, 2], F32)
            scT, scT_b = S.sb(es, 'scT', [128, KC, 2], F32)
            for g in range(2):
                S.dma('sp', out=cT[:, :, g:g + 1], in_=cvec[g:g + 1, :].rearrange("g (k p) -> p k g", p=128),
                      writes=[cT_b], allow_slow_non_contiguous=True)
            S.op('act', [cT_b], [scT_b], lambda e: e.activation(out=scT[:], in_=cT[:], func=AF.Silu))
            CB = 2048
            wring = Ring([S.sb(es, 'wada', [128, CB], F32) for _ in range(3)])
            pring = Ring([S.ps(es, 'pada', [2, CB]) for _ in range(2)])
            bring = Ring([S.sb(es, 'bada', [2, CB], F32) for _ in range(2)])
            mring = Ring([S.sb(es, 'mrow', [2, CB], F32) for _ in range(2)])
            for l in range(DEPTH):
                for cb in range(6 * D // CB):
                    pt, pt_b = pring.next()
                    bt, bt_b = bring.next()
                    for g in range(2):
                        S.dma('sp', out=bt[g:g + 1, :], in_=b_ada[l:l + 1, cb * CB:(cb + 1) * CB], writes=[bt_b])
                    for k in range(KC):
                        wt, wt_b = wring.next()
                        S.dma('sp', out=wt[:], in_=w_ada[l, k * 128:(k + 1) * 128, cb * CB:(cb + 1) * CB], writes=[wt_b])

                        def mm(e, wt=wt, pt=pt, k=k):
                            for j in range(CB // 512):
                                ins = e.matmul(pt[:, j * 512:(j + 1) * 512], lhsT=scT[:, k, :], rhs=wt[:, j * 512:(j + 1) * 512],
                                               start=(k == 0), stop=(k == KC - 1))
                            return ins
                        S.op('pe', [scT_b, wt_b], [pt_b], mm)
                    mr, mr_b = mring.next()
                    S.op('dve', [pt_b, bt_b], [mr_b], lambda e, mr=mr, pt=pt, bt=bt: e.tensor_tensor(out=mr[:], in0=pt[:], in1=bt[:], op=ALU.add))
                    S.dma('sp', out=mod_d[l, :, cb * CB:(cb + 1) * CB], in_=mr[:], reads=[mr_b])
            S.flush()

        with ExitStack() as es:
            for l in range(DEPTH):
                mt, mt_b = modT[l]
                for i in range(6):
                    for g in range(2):
                        S.dma('sp', out=mt[:, i, :, g:g + 1],
                              in_=mod_d[l, g:g + 1, i * D:(i + 1) * D].rearrange("g (k p) -> p k g", p=128),
                              writes=[mt_b], allow_slow_non_contiguous=True)
                for i in (1, 4):
                    S.op('dve', [mt_b], [mt_b], lambda e, mt=mt, i=i: e.tensor_scalar(out=mt[:, i, :, :], in0=mt[:, i, :, :], scalar1=1.0, scalar2=None, op0=ALU.add))
            S.flush()

        class HT:
            def __init__(h, es):
                h.pring = Ring([S.ps(es, 'ptr', [128, 4, 128]) for _ in range(3)])
                h.hst, h.hst_b = S.sb(es, 'hst', [128, KC, 512], BF16)
                h.flip = 0

            def emit(h, xt, xt_b, tb, l, i):
                g = 0 if tb < S_ // 128 else 1
                mt, mt_b = modT[l]
                sub = tb % 4
                for cg in range(KC // 4):
                    pt, pt_b = h.pring.next()

                    def tr(e, pt=pt, cg=cg):
                        for j in range(4):
                            ins = e.transpose(out=pt[:, j, :], in_=xt[:, (cg * 4 + j) * 128:(cg * 4 + j + 1) * 128], identity=ident[:])
                        return ins
                    S.op('pe', [xt_b, ident_b], [pt_b], tr)
                    for j in range(4):
                        k = cg * 4 + j
                        o = h.hst[:, k, sub * 128:(sub + 1) * 128]
                        sc = mt[:, 3 * i + 1, k, g:g + 1]; sh = mt[:, 3 * i, k, g:g + 1]
                        h.flip ^= 1
                        if h.flip:
                            S.op('act', [pt_b, mt_b], [h.hst_b], lambda e, o=o, pt=pt, j=j, sc=sc, sh=sh:
                                 e.activation(out=o, in_=pt[:, j, :], func=AF.Identity, scale=sc, bias=sh))
                        else:
                            S.op('dve', [pt_b, mt_b], [h.hst_b], lambda e, o=o, pt=pt, j=j, sc=sc, sh=sh:
                                 e.tensor_scalar(out=o, in0=pt[:, j, :], scalar1=sc, scalar2=sh, op0=ALU.mult, op1=ALU.add))
                if sub == 3:
                    S.dma('sp', out=hT_d[tb // 4], in_=h.hst[:], reads=[h.hst_b])

        with ExitStack() as es:
            ht = HT(es)
            xring = Ring([S.sb(es, 'xt', [128, D], F32) for _ in range(3)])
            for tb in range(c.TB):
                xt, xt_b = xring.next()
                S.dma('sp', out=xt[:], in_=xrows('in', tb), writes=[xt_b])
                ht.emit(xt, xt_b, tb, 0, 0)
            S.flush()

        def run_gemm(es, W, slabs, act_d, KCg, TT, NTT, NW, compute, kgroup=4):
            wr = [S.sb(es, 'wslab', [128, KCg, NW], BF16) for _ in range(2)]
            ar = [S.sb(es, 'atile', [128, KCg, TT], BF16) for _ in range(2)]
            items = [(si, tt) for si in range(len(slabs)) for tt in range(NTT)]

            def load_w(si):
                wb, wb_b = wr[si % 2]
                for k0 in range(0, KCg, kgroup):
                    kn = min(kgroup, KCg - k0)
                    for (c0, w, d0) in slabs[si]['segs']:
                        S.dma('pool', out=wb[:, k0:k0 + kn, d0:d0 + w],
                              in_=W[k0 * 128:(k0 + kn) * 128, c0:c0 + w].rearrange("(c p) n -> p c n", p=128), writes=[wb_b])

            def load_a(n):
                si, tt = items[n]
                ab, ab_b = ar[n % 2]
                S.dma('sp', out=ab[:], in_=act_d[tt], writes=[ab_b])

            load_w(0); load_a(0)
            for n, (si, tt) in enumerate(items):
                if tt == 0 and si + 1 < len(slabs):
                    load_w(si + 1)
                if n + 1 < len(items):
                    load_a(n + 1)
                wb, wb_b = wr[si % 2]; ab, ab_b = ar[n % 2]
                compute(slabs[si], tt, wb, wb_b, ab, ab_b)

        def fm_group(pb, pb_b, wb, wb_b, ab, ab_b, KCg, ci, ks=None, ncols=512):
            ks = list(range(KCg)) if ks is None else ks

            def mm(e):
                for n_, k in enumerate(ks):
                    ins = e.matmul(pb[:, 0:ncols], lhsT=wb[:, k, ci * 128:(ci + 1) * 128], rhs=ab[:, k, 0:ncols],
                                   start=(n_ == 0), stop=(n_ == len(ks) - 1))
                return ins
            S.op('pe', [wb_b, ab_b], [pb_b], mm)

        def tm_group(pb, pb_b, wb, wb_b, ab, ab_b, KCg, sub, w):
            def mm(e):
                for k in range(KCg):
                    ins = e.matmul(pb[:, 0:w], lhsT=ab[:, k, sub * 128:(sub + 1) * 128], rhs=wb[:, k, 0:w],
                                   start=(k == 0), stop=(k == KCg - 1))
                return ins
            S.op('pe', [wb_b, ab_b], [pb_b], mm)

        for l in range(DEPTH):
            lam_init = 0.8 - 0.6 * math.exp(-0.3 * l)
            import os as _os
            with ExitStack() as es:
                secs = [('qa', c.MIXA), ('ka', c.MIXA), ('va', c.MIXA), ('qb', c.MIXB), ('kb', c.MIXB), ('vb', c.MIXB),
                        ('qc', c.MIXC), ('kc', c.KVW), ('vc', c.KVW), ('g0', D), ('g1', D), ('g2', D)]
                slabs = []
                col = 0
                for (nm, wd) in secs:
                    for off in range(0, wd, 512):
                        w = min(512, wd - off)
                        slabs.append(dict(sec=nm, off=off, w=w, segs=[(col + off, w, 0)]))
                    col += wd
                pring = Ring([S.ps(es, 'pq', [128, 512]) for _ in range(5)])
                pring2 = Ring([S.ps(es, 'pq2', [128, 512]) for _ in range(2)])
                ostr = Ring([S.sb(es, 'ost', [128, 4, 512], BF16) for _ in range(3)])
                fstr = Ring([S.sb(es, 'fst', [128, 4, 512], F32) for _ in range(2)])
                xbr = Ring([S.sb(es, 'xb', [128, 512], BF16) for _ in range(3)])
                t1r = Ring([S.sb(es, 't1', [128, 512], F32) for _ in range(3)])
                t2r = Ring([S.sb(es, 't2', [128, 512], F32) for _ in range(3)])
                cosr = Ring([S.sb(es, 'cos', [128, 512], F32) for _ in range(2)])
                sinr = Ring([S.sb(es, 'sin', [128, 512], F32) for _ in range(2)])
                permB, permB_b = S.sb(es, 'permB', [128, 128], BF16)
                permC, permC_b = S.sb(es, 'permC', [128, 128], BF16)
                S.dma('pool', out=permB[:], in_=ropeB[2, :, 0:128], writes=[permB_b])
                S.dma('pool', out=permC[:], in_=ropeC[2, :, 0:128], writes=[permC_b])
                kout = {'ka': (nak, c.CKA), 'kb': (nbk, c.CKB), 'kc': (nck, c.CKC)}
                qkbase = {'qa': c.CQA, 'ka': c.CKA, 'qb': c.CQB, 'kb': c.CKB, 'kc': c.CKC}
                vout = {'va': (nav, 0), 'vb': (nbv, c.MIXA), 'vc': (ncv, c.MIXA + c.MIXB)}

                def store_rows(dst, off, w, fst, fst_b):
                    for b in range(c.NP):
                        S.dma('sp', out=dst[b, l, :, off:off + w].rearrange("(h p) w -> p h w", p=128),
                              in_=fst[:, 2 * b:2 * b + 2, 0:w], reads=[fst_b])

                def qkv_compute(slab, tt, wb, wb_b, ab, ab_b):
                    sec, off, w = slab['sec'], slab['off'], slab['w']
                    nch = w // 128
                    if sec in vout:
                        dst, voff = vout[sec]
                        ost, ost_b = ostr.next()
                        fst = fst_b = None
                        if tt == NT - 1:
                            fst, fst_b = fstr.next()
                        for sub in range(4):
                            pb, pb_b = pring.next()
                            tm_group(pb, pb_b, wb, wb_b, ab, ab_b, KC, sub, w)
                            copy_evac(ost[:, sub, 0:w], pb[:, 0:w], [pb_b], [ost_b])
                            if fst is not None:
                                copy_evac(fst[:, sub, 0:w], pb[:, 0:w], [pb_b], [fst_b])
                        if 'novd' not in _os.environ.get('KDBG', ''):
                            S.dma('sp', out=V_d[tt * 4:(tt + 1) * 4, :, voff + off:voff + off + w].rearrange("s p w -> p s w"),
                                  in_=ost[:, :, 0:w], reads=[ost_b])
                        if fst is not None and 'noout' not in _os.environ.get('KDBG', ''):
                            store_rows(dst, off, w, fst, fst_b)
                        return
                    rope = sec in ('qb', 'kb', 'qc', 'kc') and tt < NTS
                    if rope:
                        tab = ropeB if sec in ('qb', 'kb') else ropeC
                        perm, perm_b = (permB, permB_b) if sec in ('qb', 'kb') else (permC, permC_b)
                        ct, ct_b = cosr.next(); st, st_b = sinr.next()
                        S.dma('sp', out=ct[:], in_=tab[0, :, tt * 512:(tt + 1) * 512], writes=[ct_b])
                        S.dma('sp', out=st[:], in_=tab[1, :, tt * 512:(tt + 1) * 512], writes=[st_b])
                    ost, ost_b = ostr.next()
                    pending = []
                    for ci in range(nch):
                        pb, pb_b = pring.next()
                        fm_group(pb, pb_b, wb, wb_b, ab, ab_b, KC, ci)
                        while pending:
                            pending.pop()()
                        if sec[0] == 'g':
                            S.op('act', [pb_b], [ost_b], lambda e, o=ost[:, ci, :], pb=pb: e.activation(out=o, in_=pb[:], func=AF.Sigmoid))
                        elif rope:
                            xb, xb_b = xbr.next(); t1, t1_b = t1r.next(); t2, t2_b = t2r.next()
                            S.op('act', [pb_b], [xb_b], lambda e, xb=xb, pb=pb: e.activation(out=xb[:], in_=pb[:], func=AF.Copy))
                            S.op('dve', [pb_b, ct_b], [t1_b], lambda e, t1=t1, pb=pb, ct=ct: e.tensor_tensor(out=t1[:], in0=pb[:], in1=ct[:], op=ALU.mult))

                            def later(xb=xb, xb_b=xb_b, t1=t1, t1_b=t1_b, t2=t2, t2_b=t2_b, ci=ci, st=st, st_b=st_b, perm=perm, perm_b=perm_b):
                                p2, p2_b = pring2.next()
                                S.op('pe', [xb_b, perm_b], [p2_b], lambda e: e.matmul(p2[:], lhsT=perm[:], rhs=xb[:], start=True, stop=True))
                                S.op('dve', [p2_b, st_b], [t2_b], lambda e: e.tensor_tensor(out=t2[:], in0=p2[:], in1=st[:], op=ALU.mult))
                                S.op('dve', [t1_b, t2_b], [ost_b], lambda e: e.tensor_tensor(out=ost[:, ci, :], in0=t1[:], in1=t2[:], op=ALU.add))
                            pending.append(later)
                        else:
                            copy_evac(ost[:, ci, :], pb[:], [pb_b], [ost_b])
                    while pending:
                        pending.pop()()
                    if sec[0] == 'g':
                        br = int(sec[1]); fs = off // 512
                        S.dma('sp', out=G_d[tt, fs, :, br, :, :], in_=ost[:], reads=[ost_b])
                    elif sec == 'qc':
                        n = off // 512
                        for j in range(4):
                            S.dma('sp', out=QC_d[n, :, tt * 4 + j, :, :], in_=ost[:, :, j * 128:(j + 1) * 128], reads=[ost_b])
                    else:
                        c0 = qkbase[sec] + off // 128
                        S.dma('sp', out=QK_d[c0:c0 + nch, :, tt * 512:(tt + 1) * 512].rearrange("c p t -> p c t"),
                              in_=ost[:, 0:nch, :], reads=[ost_b])
                    if sec in kout and tt == NT - 1:
                        dst, _ = kout[sec]
                        fst, fst_b = fstr.next()
                        for sub in range(4):
                            pb, pb_b = pring.next()
                            tm_group(pb, pb_b, wb, wb_b, ab, ab_b, KC, sub, w)
                            copy_evac(fst[:, sub, 0:w], pb[:, 0:w], [pb_b], [fst_b])
                        store_rows(dst, off, w, fst, fst_b)

                import os as _os
                _ks = _os.environ.get('KSLAB')
                if _ks:
                    slabs = [slabs[int(x)] for x in _ks.split(',')]
                run_gemm(es, w_in[l], slabs, hT_d, KC, 512, NT, 512, qkv_compute)
                S.flush()

            with ExitStack() as es:
                HAL = c.HA + c.HB + c.KVC
                ckT, ckT_b = S.sb(es, 'ckT', [128, HAL, c.PAST], BF16)
                cvt, cvt_b = S.sb(es, 'cvt', [128, c.PB, c.VW], BF16)
                for (src, off, w) in ((cav, 0, c.MIXA), (cbv, c.MIXA, c.MIXB), (ccv, c.MIXA + c.MIXB, c.KVW)):
                    for w0 in range(0, w, 512):
                        ww = min(512, w - w0)
                        S.dma('pool', out=cvt[:, :, off + w0:off + w0 + ww],
                              in_=src[l, :, w0:w0 + ww].rearrange("(j p) w -> p j w", p=128), writes=[cvt_b])
                pS = Ring([S.ps(es, 'pS', [128, 512]) for _ in range(4)])
                pO = [S.ps(es, 'pO', [128, 512]) for _ in range(4)]
                ckf_r = Ring([S.sb(es, 'ckf', [128, 1024], F32) for _ in range(2)])
                hbase = 0
                for (src, nh) in ((cak, c.HA), (cbk, c.HB), (cck, c.KVC)):
                    for j in range(c.PB):
                        for h0 in range(0, nh, 8):
                            hn = min(8, nh - h0)
                            ckf, ckf_b = ckf_r.next()
                            S.dma('sp', out=ckf[:, 0:hn * 128], in_=src[l, j * 128:(j + 1) * 128, h0 * 128:(h0 + hn) * 128], writes=[ckf_b])
                            for h1 in range(0, hn, 4):
                                h2 = min(4, hn - h1)
                                pt, pt_b = pS.next()

                                def tr(e, pt=pt, ckf=ckf, h1=h1, h2=h2):
                                    for jj in range(h2):
                                        ins = e.transpose(out=pt[:, jj * 128:(jj + 1) * 128], in_=ckf[:, (h1 + jj) * 128:(h1 + jj + 1) * 128], identity=ident[:])
                                    return ins
                                S.op('pe', [ckf_b, ident_b], [pt_b], tr)
                                hh = hbase + h0 + h1
                                copy_evac(ckT[:, hh:hh + h2, j * 128:(j + 1) * 128],
                                          pt[:, 0:h2 * 128].rearrange("p (a b) -> p a b", b=128), [pt_b], [ckT_b])
                    hbase += nh
                lamt, lamt_b = S.sb(es, 'lamt', [128, 256], F32)
                lsc, lsc_b = S.sb(es, 'lsc', [128, 8], F32)
                gainc, gainc_b = S.sb(es, 'gainc', [128, 1], F32)
                esink, esink_b = S.sb(es, 'esink', [128, c.HC], F32)
                S.dma('sp', out=lamt[:], in_=lam_b[l:l + 1, :].to_broadcast([128, 256]), writes=[lamt_b])
                S.dma('sp', out=esink[:], in_=sink[l:l + 1, :].to_broadcast([128, c.HC]), writes=[esink_b])
                S.dma('sp', out=gainc[:], in_=subln[l:l + 1, :].rearrange("o p -> p o"), writes=[gainc_b], allow_slow_non_contiguous=True)
                S.op('act', [esink_b], [esink_b], lambda e: e.activation(out=esink[:], in_=esink[:], func=AF.Exp))
                S.op('dve', [gainc_b], [gainc_b], lambda e: e.tensor_scalar(out=gainc[:], in0=gainc[:], scalar1=(1.0 - lam_init), scalar2=None, op0=ALU.mult))
                S.op('dve', [lamt_b], [lamt_b], lambda e: e.tensor_tensor(out=lamt[:, 0:64], in0=lamt[:, 0:64], in1=lamt[:, 64:128], op=ALU.mult))
                S.op('dve', [lamt_b], [lamt_b], lambda e: e.tensor_tensor(out=lamt[:, 128:192], in0=lamt[:, 128:192], in1=lamt[:, 192:256], op=ALU.mult))
                S.op('dve', [lamt_b], [lsc_b], lambda e: e.tensor_reduce(out=lsc[:, 0:1], in_=lamt[:, 0:64], axis=AX.X, op=ALU.add))
                S.op('dve', [lamt_b, lsc_b], [lsc_b], lambda e: e.tensor_reduce(out=lsc[:, 1:2], in_=lamt[:, 128:192], axis=AX.X, op=ALU.add))
                S.op('act', [lsc_b], [lsc_b], lambda e: e.activation(out=lsc[:, 2:4], in_=lsc[:, 0:2], func=AF.Exp))
                S.op('dve', [lsc_b], [lsc_b], lambda e: e.tensor_tensor(out=lsc[:, 4:5], in0=lsc[:, 3:4], in1=lsc[:, 2:3], op=ALU.subtract))
                S.op('dve', [lsc_b], [lsc_b], lambda e: e.tensor_scalar(out=lsc[:, 5:6], in0=lsc[:, 4:5], scalar1=-lam_init, scalar2=None, op0=ALU.add))
                neglam = lsc[:, 5:6]
                trimt, trimt_b = S.sb(es, 'trimt', [128, 2, 128], BF16)
                S.dma('pool', out=trimt[:], in_=trim.rearrange("a p q -> p a q"), writes=[trimt_b])

                pTr = Ring([S.sb(es, 'pT', [128, 512], BF16) for _ in range(6)])
                rdr = Ring([S.sb(es, 'rd', [128, 512], F32) for _ in range(3)])
                f1r = Ring([S.sb(es, 'f1', [128, 512], F32) for _ in range(3)])
                f2r = Ring([S.sb(es, 'f2', [128, 512], F32) for _ in range(3)])

                def load_kv(kt, kt_b, vt, vt_b, chunk, voff, t0, nblk):
                    S.dma('sp', out=kt[:, 0:nblk * 128], in_=QK_d[chunk, :, t0:t0 + nblk * 128], writes=[kt_b])
                    S.dma('sp', out=vt[:, 0:nblk, :], in_=V_d[t0 // 128:t0 // 128 + nblk, :, voff:voff + 128].rearrange("b p w -> p b w"),
                          writes=[vt_b])

                def attn_single(groups, nq, qaps, scale, extra_den, out_ap, out_b, kbufs, oshp=(lambda a: a)):
                    po, po_b = pO[0]; pd, pd_b = pO[1]
                    nblk_total = sum(len(g[0]) for g in groups)
                    seen = [0]
                    pend = []

                    def pv_stage(blocks, pt, pt_b):
                        def mm(e):
                            for bi, (kT_ap, v_ap) in enumerate(blocks):
                                first = (seen[0] == 0); seen[0] += 1; last = (seen[0] == nblk_total)
                                e.matmul(po[:, 0:nq], lhsT=v_ap, rhs=pt[:, bi * nq:(bi + 1) * nq], start=first, stop=last)
                                ins = e.matmul(pd[:, 0:nq], lhsT=ones_bf[:], rhs=pt[:, bi * nq:(bi + 1) * nq], start=first, stop=last)
                            return ins
                        S.op('pe', [pt_b, ones_bf_b] + kbufs, [po_b, pd_b], mm)

                    for (blocks, post) in groups:
                        ps_, ps_b = pS.next()
                        ncol = len(blocks) * nq

                        def qk(e, blocks=blocks, ps_=ps_):
                            for bi, (kT_ap, v_ap) in enumerate(blocks):
                                ins = e.matmul(ps_[:, bi * nq:(bi + 1) * nq], lhsT=kT_ap, rhs=qaps, start=True, stop=True)
                            return ins
                        S.op('pe', kbufs, [ps_b], qk)
                        while pend:
                            pend.pop()()
                        pt, pt_b = pTr.next()
                        S.op('act', [ps_b], [pt_b], lambda e, pt=pt, ps_=ps_, ncol=ncol: e.activation(out=pt[:, 0:ncol], in_=ps_[:, 0:ncol], func=AF.Exp, scale=scale))
                        if post is not None:
                            post_ap, post_b = post
                            S.op('dve', [pt_b, post_b], [pt_b], lambda e, pt=pt, ncol=ncol, post_ap=post_ap: e.tensor_tensor(
                                out=post_ap[0](pt[:, 0:ncol]), in0=post_ap[0](pt[:, 0:ncol]), in1=post_ap[1], op=ALU.mult))
                        pend.append(lambda blocks=blocks, pt=pt, pt_b=pt_b: pv_stage(blocks, pt, pt_b))
                    while pend:
                        pend.pop()()
                    rd, rd_b = rdr.next()
                    if extra_den is not None:
                        ed_ap, ed_b, shp = extra_den
                        S.op('dve', [pd_b, ed_b], [rd_b], lambda e: e.tensor_tensor(out=shp(rd[:, 0:nq]), in0=shp(pd[:, 0:nq]), in1=ed_ap, op=ALU.add))
                        S.op('dve', [rd_b], [rd_b], lambda e: e.reciprocal(out=rd[:, 0:nq], in_=rd[:, 0:nq]))
                    else:
                        S.op('dve', [pd_b], [rd_b], lambda e: e.reciprocal(out=rd[:, 0:nq], in_=pd[:, 0:nq]))
                    S.op('dve', [po_b, rd_b], [out_b], lambda e: e.tensor_tensor(out=out_ap, in0=oshp(po[:, 0:nq]), in1=oshp(rd[:, 0:nq]), op=ALU.mult))

                def attn_diff(blocks, nq, qt, scale, out_ap, out_b, kbufs):
                    (o1, o1_b), (d1, d1_b), (o2, o2_b), (d2, d2_b) = pO
                    nb = len(blocks)
                    pend = []

                    def pv_stage(bi, v_ap, p1, p1_b, p2, p2_b):
                        def mm(e):
                            e.matmul(o1[:, 0:nq], lhsT=v_ap, rhs=p1[:, 0:nq], start=(bi == 0), stop=(bi == nb - 1))
                            e.matmul(d1[:, 0:nq], lhsT=ones_bf[:], rhs=p1[:, 0:nq], start=(bi == 0), stop=(bi == nb - 1))
                            e.matmul(o2[:, 0:nq], lhsT=v_ap, rhs=p2[:, 0:nq], start=(bi == 0), stop=(bi == nb - 1))
                            return e.matmul(d2[:, 0:nq], lhsT=ones_bf[:], rhs=p2[:, 0:nq], start=(bi == 0), stop=(bi == nb - 1))
                        S.op('pe', [p1_b, p2_b, ones_bf_b] + kbufs, [o1_b, d1_b, o2_b, d2_b], mm)

                    for bi, (kT_ap, v_ap) in enumerate(blocks):
                        s1, s1_b = pS.next(); s2, s2_b = pS.next()

                        def qk(e, kT_ap=kT_ap, s1=s1, s2=s2):
                            e.matmul(s1[:, 0:nq], lhsT=kT_ap[0:64, :], rhs=qt[0:64, :], start=True, stop=True)
                            return e.matmul(s2[:, 0:nq], lhsT=kT_ap[64:128, :], rhs=qt[64:128, :], start=True, stop=True)
                        S.op('pe', kbufs, [s1_b, s2_b], qk)
                        while pend:
                            pend.pop()()
                        p1, p1_b = pTr.next(); p2, p2_b = pTr.next()
                        S.op('act', [s1_b], [p1_b], lambda e, p1=p1, s1=s1: e.activation(out=p1[:, 0:nq], in_=s1[:, 0:nq], func=AF.Exp, scale=scale))
                        S.op('act', [s2_b], [p2_b], lambda e, p2=p2, s2=s2: e.activation(out=p2[:, 0:nq], in_=s2[:, 0:nq], func=AF.Exp, scale=scale))
                        pend.append(lambda bi=bi, v_ap=v_ap, p1=p1, p1_b=p1_b, p2=p2, p2_b=p2_b: pv_stage(bi, v_ap, p1, p1_b, p2, p2_b))
                    while pend:
                        pend.pop()()
                    rd, rd_b = rdr.next(); f1, f1_b = f1r.next(); f2, f2_b = f2r.next()
                    S.op('dve', [d1_b], [rd_b], lambda e: e.reciprocal(out=rd[:, 0:nq], in_=d1[:, 0:nq]))
                    S.op('dve', [o1_b, rd_b], [f1_b], lambda e: e.tensor_tensor(out=f1[:, 0:nq], in0=o1[:, 0:nq], in1=rd[:, 0:nq], op=ALU.mult))
                    S.op('dve', [d2_b, rd_b], [rd_b], lambda e: e.reciprocal(out=rd[:, 0:nq], in_=d2[:, 0:nq]))
                    S.op('dve', [o2_b, rd_b], [f2_b], lambda e: e.tensor_tensor(out=f2[:, 0:nq], in0=o2[:, 0:nq], in1=rd[:, 0:nq], op=ALU.mult))
                    S.op('dve', [f1_b, f2_b, lsc_b], [f1_b], lambda e: e.scalar_tensor_tensor(out=f1[:, 0:nq], in0=f2[:, 0:nq], scalar=neglam, in1=f1[:, 0:nq],
                                                                                                 op0=ALU.mult, op1=ALU.add))
                    S.op('act', [f1_b], [f2_b], lambda e: e.activation(out=f2[:, 0:nq], in_=f1[:, 0:nq], func=AF.Square))
                    sq, sq_b = pS.next()
                    S.op('pe', [f2_b, ones_f_b], [sq_b], lambda e: e.matmul(sq[:, 0:nq], lhsT=ones_f[:], rhs=f2[:, 0:nq], start=True, stop=True))
                    S.op('act', [sq_b, eps_rms_b], [rd_b], lambda e: e.activation(out=rd[:, 0:nq], in_=sq[:, 0:nq], func=AF.Sqrt, scale=1.0 / 128, bias=eps_rms[:]))
                    S.op('dve', [rd_b], [rd_b], lambda e: e.reciprocal(out=rd[:, 0:nq], in_=rd[:, 0:nq]))
                    S.op('dve', [f1_b, rd_b, gainc_b], [out_b], lambda e: e.scalar_tensor_tensor(out=out_ap, in0=f1[:, 0:nq], scalar=gainc[:, 0:1], in1=rd[:, 0:nq],
                                                                                                   op0=ALU.mult, op1=ALU.mult))

                sc128 = 128 ** -0.5
                sc64 = 64 ** -0.5
                nsb = c.NB
                ktr = Ring([S.sb(es, 'kt', [128, S_], BF16) for _ in range(2)])
                vtr = Ring([S.sb(es, 'vt', [128, nsb, 128], BF16) for _ in range(2)])
                qtr = Ring([S.sb(es, 'qt', [128, S_], BF16) for _ in range(2)])
                osr = Ring([S.sb(es, 'os', [128, 512], BF16) for _ in range(3)])
                opr = Ring([S.sb(es, 'op', [128, 512], BF16) for _ in range(2)])

                def store_o(os_, os_b, chunk, tt):
                    S.dma('sp', out=O_d[tt, :, chunk, :], in_=os_[:], reads=[os_b])

                Et, Et_b = S.sb(es, 'Et', [128, NV, 128], BF16)
                Ef, Ef_b = S.sb(es, 'Ef', [128, NV, 128], F32)
                for h in range(c.HA):
                    kt, kt_b = ktr.next(); vt, vt_b = vtr.next(); qt, qt_b = qtr.next()
                    load_kv(kt, kt_b, vt, vt_b, c.CKA + h, h * 128, 0, nsb)
                    S.dma('sp', out=qt[:], in_=QK_d[c.CQA + h, :, 0:S_], writes=[qt_b])
                    S.dma('sp', out=Ef[:], in_=rpbt[l, h].rearrange("v k q -> k v q"), writes=[Ef_b])
                    S.op('act', [Ef_b], [Et_b], lambda e: e.activation(out=Et[:], in_=Ef[:], func=AF.Exp))
                    kb_ = [kt_b, vt_b, qt_b, ckT_b, cvt_b]
                    ctx_blocks = [(ckT[:, h, j * 128:(j + 1) * 128], cvt[:, j, h * 128:(h + 1) * 128]) for j in range(c.PB)]
                    for i in range(nsb):
                        if i % 4 == 0:
                            os_, os_b = osr.next()
                        ms, v0 = blocks_na[i]
                        groups = [(ctx_blocks, None)]
                        for g0 in range(0, len(ms), 4):
                            mm_ = ms[g0:g0 + 4]
                            blks = [(kt[:, m * 128:(m + 1) * 128], vt[:, m, :]) for m in mm_]
                            ev = Et[:, v0 + g0:v0 + g0 + len(mm_), :]
                            groups.append((blks, (((lambda a: a.rearrange("p (a b) -> p a b", b=128)), ev), Et_b)))
                        attn_single(groups, 128, qt[:, i * 128:(i + 1) * 128], sc128, None, os_[:, (i % 4) * 128:(i % 4 + 1) * 128], os_b, kb_)
                        if i % 4 == 3:
                            store_o(os_, os_b, h, i // 4)
                    kp, kp_b = ktr.next(); vp, vp_b = vtr.next(); op_, op_b = opr.next()
                    load_kv(kp, kp_b, vp, vp_b, c.CKA + h, h * 128, S_, 4)
                    S.dma('sp', out=kp[:, 512:1024], in_=QK_d[c.CQA + h, :, S_:S_ + 512], writes=[kp_b])
                    for b in range(c.NP):
                        blks = [(kp[:, (2 * b + j) * 128:(2 * b + j + 1) * 128], vp[:, 2 * b + j, :]) for j in range(2)]
                        attn_single([(blks, None)], 256, kp[:, 512 + b * 256:512 + (b + 1) * 256], sc128, None,
                                    op_[:, b * 256:(b + 1) * 256], op_b, [kp_b, vp_b])
                    S.dma('sp', out=O_d[NTS, :, h, :], in_=op_[:], reads=[op_b])

                for h in range(c.HB):
                    kt, kt_b = ktr.next(); vt, vt_b = vtr.next(); qt, qt_b = qtr.next()
                    load_kv(kt, kt_b, vt, vt_b, c.CKB + h, c.MIXA + h * 128, 0, nsb)
                    S.dma('sp', out=qt[:], in_=QK_d[c.CQB + h, :, 0:S_], writes=[qt_b])
                    kb_ = [kt_b, vt_b, qt_b, ckT_b, cvt_b]
                    blks = [(kt[:, m * 128:(m + 1) * 128], vt[:, m, :]) for m in range(nsb)]
                    blks += [(ckT[:, c.HA + h, j * 128:(j + 1) * 128], cvt[:, j, c.MIXA + h * 128:c.MIXA + (h + 1) * 128]) for j in range(c.PB)]
                    for qi in range(NTS):
                        os_, os_b = osr.next()
                        attn_diff(blks, 512, qt[:, qi * 512:(qi + 1) * 512], sc64, os_[:], os_b, kb_)
                        store_o(os_, os_b, c.HA + h, qi)
                    kp, kp_b = ktr.next(); vp, vp_b = vtr.next(); op_, op_b = opr.next()
                    load_kv(kp, kp_b, vp, vp_b, c.CKB + h, c.MIXA + h * 128, S_, 4)
                    S.dma('sp', out=kp[:, 512:1024], in_=QK_d[c.CQB + h, :, S_:S_ + 512], writes=[kp_b])
                    for b in range(c.NP):
                        blks = [(kp[:, (2 * b + j) * 128:(2 * b + j + 1) * 128], vp[:, 2 * b + j, :]) for j in range(2)]
                        attn_diff(blks, 256, kp[:, 512 + b * 256:512 + (b + 1) * 256], sc64, op_[:, b * 256:(b + 1) * 256], op_b, [kp_b, vp_b])
                    S.dma('sp', out=O_d[NTS, :, c.HA + h, :], in_=op_[:], reads=[op_b])

                qcr = Ring([S.sb(es, 'qc', [128, c.TB, 4, 128], BF16) for _ in range(1)])
                ocr = Ring([S.sb(es, 'oc', [128, 4, 512], BF16) for _ in range(2)])
                b3 = lambda a: a.rearrange("p (g q) -> p g q", g=4)
                for n in range(c.KVC):
                    kt, kt_b = ktr.next(); vt, vt_b = vtr.next()
                    qc_, qc_b = qcr.next()
                    load_kv(kt, kt_b, vt, vt_b, c.CKC + n, c.MIXA + c.MIXB + n * 128, 0, nsb)
                    S.dma('sp', out=qc_[:], in_=QC_d[n], writes=[qc_b])
                    kp, kp_b = ktr.next(); vp, vp_b = vtr.next()
                    load_kv(kp, kp_b, vp, vp_b, c.CKC + n, c.MIXA + c.MIXB + n * 128, S_, 4)
                    kb_ = [kt_b, vt_b, qc_b, ckT_b, cvt_b, kp_b, vp_b]
                    hh = c.HA + c.HB + n
                    ctx_blocks = [(ckT[:, hh, j * 128:(j + 1) * 128], cvt[:, j, c.MIXA + c.MIXB + n * 128:c.MIXA + c.MIXB + (n + 1) * 128]) for j in range(c.PB)]
                    es_ap = esink[:, n * 4:(n + 1) * 4].unsqueeze(2).to_broadcast([128, 4, 128])
                    for i in range(c.TB):
                        if i % 4 == 0:
                            oc, oc_b = ocr.next()
                        groups = []
                        if i < nsb:
                            for m in (i - 1, i, i + 1):
                                if m < 0 or m >= nsb:
                                    continue
                                post = None
                                if m != i:
                                    mk = trimt[:, 0 if m < i else 1, :].unsqueeze(1).to_broadcast([128, 4, 128])
                                    post = ((b3, mk), trimt_b)
                                groups.append(([(kt[:, m * 128:(m + 1) * 128], vt[:, m, :])], post))
                            for blk in ctx_blocks:
                                groups.append(([blk], None))
                        else:
                            b = (i - nsb) // 2
                            for j in range(2):
                                groups.append(([(kp[:, (2 * b + j) * 128:(2 * b + j + 1) * 128], vp[:, 2 * b + j, :])], None))
                        qap = qc_[:, i, :, :].rearrange("p g q -> p (g q)")
                        attn_single(groups, 512, qap, sc128, (es_ap, esink_b, b3), oc[:, :, (i % 4) * 128:(i % 4 + 1) * 128], oc_b, kb_, oshp=b3)
                        if i % 4 == 3:
                            ch0 = c.HA + c.HB + n * 4
                            S.dma('sp', out=O_d[i // 4, :, ch0:ch0 + 4, :], in_=oc[:], reads=[oc_b])
                S.flush()

            with ExitStack() as es:
                slabs = [dict(fs=fs, segs=[(fs * 512, 512, 0)]) for fs in range(D // 512)]
                pA = Ring([S.ps(es, 'pA', [128, 512]) for _ in range(2)])
                pB = Ring([S.ps(es, 'pB', [128, 512]) for _ in range(2)])
                pC = Ring([S.ps(es, 'pC', [128, 512]) for _ in range(2)])
                gtr = Ring([S.sb(es, 'gt', [128, 3, 4, 512], BF16) for _ in range(2)])
                mstr = Ring([S.sb(es, 'mst', [128, 4, 512], BF16) for _ in range(2)])
                t1r = Ring([S.sb(es, 't1', [128, 512], F32) for _ in range(2)])
                t2r = Ring([S.sb(es, 't2', [128, 512], F32) for _ in range(2)])
                ka = list(range(0, c.HA)); kb2 = list(range(c.HA, c.HA + c.HB)); kc2 = list(range(c.HA + c.HB, KC))

                def br_compute(slab, tt, wb, wb_b, ab, ab_b):
                    fs = slab['fs']
                    gt, gt_b = gtr.next(); mst, mst_b = mstr.next()
                    S.dma('sp', out=gt[:], in_=G_d[tt, fs], writes=[gt_b])
                    for ci in range(4):
                        a_, a_b = pA.next(); b_, b_b = pB.next(); c_, c_b = pC.next()
                        fm_group(a_, a_b, wb, wb_b, ab, ab_b, KC, ci, ks=ka)
                        fm_group(b_, b_b, wb, wb_b, ab, ab_b, KC, ci, ks=kb2)
                        fm_group(c_, c_b, wb, wb_b, ab, ab_b, KC, ci, ks=kc2)
                        t1, t1_b = t1r.next(); t2, t2_b = t2r.next()
                        S.op('dve', [a_b, gt_b], [t1_b], lambda e, t1=t1, a_=a_, gt=gt, ci=ci: e.tensor_tensor(out=t1[:], in0=a_[:], in1=gt[:, 0, ci, :], op=ALU.mult))
                        S.op('dve', [b_b, gt_b], [t2_b], lambda e, t2=t2, b_=b_, gt=gt, ci=ci: e.tensor_tensor(out=t2[:], in0=b_[:], in1=gt[:, 1, ci, :], op=ALU.mult))
                        S.op('dve', [t1_b, t2_b], [t1_b], lambda e, t1=t1, t2=t2: e.tensor_tensor(out=t1[:], in0=t1[:], in1=t2[:], op=ALU.add))
                        S.op('dve', [c_b, gt_b, t2_b], [t2_b], lambda e, t2=t2, c_=c_, gt=gt, ci=ci: e.tensor_tensor(out=t2[:], in0=c_[:], in1=gt[:, 2, ci, :], op=ALU.mult))
                        S.op('dve', [t1_b, t2_b], [mst_b], lambda e, t1=t1, t2=t2, mst=mst, ci=ci: e.tensor_tensor(out=mst[:, ci, :], in0=t1[:], in1=t2[:], op=ALU.add))
                    S.dma('sp', out=M_d[tt, :, fs * 4:(fs + 1) * 4, :], in_=mst[:], reads=[mst_b])

                run_gemm(es, w_branch[l], slabs, O_d, KC, 512, NT, 512, br_compute)
                S.flush()

            def tm_gemm_phase(W, act_d, KCg, TT, NTT, NW):
                with ExitStack() as es:
                    slabs = [dict(n0=n0, segs=[(n0, NW, 0)]) for n0 in range(0, D, NW)]
                    pr = Ring([S.ps(es, 'py', [128, 512]) for _ in range(6)])
                    nsub = TT // 128
                    ystr = Ring([S.sb(es, 'yst', [128, nsub, NW], F32) for _ in range(3)])

                    def comp(slab, tt, wb, wb_b, ab, ab_b):
                        yst, yst_b = ystr.next()
                        for sub in range(nsub):
                            pb, pb_b = pr.next()
                            tm_group(pb, pb_b, wb, wb_b, ab, ab_b, KCg, sub, NW)
                            copy_evac(yst[:, sub, :], pb[:, 0:NW], [pb_b], [yst_b])
                        S.dma('sp', out=y_d[tt * TT:(tt + 1) * TT, slab['n0']:slab['n0'] + NW].rearrange("(s p) n -> p s n", p=128),
                              in_=yst[:], reads=[yst_b])
                    run_gemm(es, W, slabs, act_d, KCg, TT, NTT, NW, comp, kgroup=(4 if NW == 512 else 8))
                    S.flush()

            def ln_phase(xsrc, xdst, li, nxt):
                with ExitStack() as es:
                    ht = HT(es) if nxt is not None else None
                    xr = Ring([S.sb(es, 'lx', [128, D], F32) for _ in range(2)])
                    yr = Ring([S.sb(es, 'ly', [128, D], F32) for _ in range(1)])
                    zr = Ring([S.sb(es, 'lz', [128, D], F32) for _ in range(2)])
                    gbc, gbc_b = S.sb(es, 'gbc', [128, D], F32)
                    lg, lg_b = S.sb(es, 'lg', [128, D], F32)
                    lb, lb_b = S.sb(es, 'lb', [128, D], F32)
                    S.dma('sp', out=lg[:], in_=ln_g[l, li:li + 1, :].to_broadcast([128, D]), writes=[lg_b])
                    S.dma('sp', out=lb[:], in_=ln_b[l, li:li + 1, :].to_broadcast([128, D]), writes=[lb_b])
                    nst = D // 512 if D >= 512 else 1
                    fw = D // nst
                    str_ = Ring([S.sb(es, 'st', [128, nst, 6], F32) for _ in range(2)])
                    mvr = Ring([S.sb(es, 'mv', [128, 4], F32) for _ in range(2)])
                    gi = 3 * li + 2
                    for tb in range(c.TB):
                        g = 0 if tb < S_ // 128 else 1
                        if tb == 0 or tb == S_ // 128:
                            S.dma('sp', out=gbc[:], in_=mod_d[l, g:g + 1, gi * D:(gi + 1) * D].to_broadcast([128, D]), writes=[gbc_b])
                        xt, xt_b = xr.next(); yt, yt_b = yr.next(); zt, zt_b = zr.next()
                        st, st_b = str_.next(); mv, mv_b = mvr.next()
                        S.dma('sp', out=xt[:], in_=xrows(xsrc, tb), writes=[xt_b])
                        S.dma('sp', out=yt[:], in_=y_d[tb * 128:(tb + 1) * 128, :], writes=[yt_b])
                        S.op('dve', [yt_b, gbc_b], [yt_b], lambda e, yt=yt: e.tensor_tensor(out=yt[:], in0=yt[:], in1=gbc[:], op=ALU.mult))
                        S.op('dve', [xt_b, yt_b], [xt_b], lambda e, xt=xt, yt=yt: e.scalar_tensor_tensor(out=xt[:], in0=xt[:], scalar=c.ALPHA, in1=yt[:], op0=ALU.mult, op1=ALU.add))

                        def stats(e, xt=xt, st=st):
                            for j in range(nst):
                                ins = e.bn_stats(out=st[:, j, :], in_=xt[:, j * fw:(j + 1) * fw])
                            return ins
                        S.op('dve', [xt_b], [st_b], stats)
                        S.op('dve', [st_b], [mv_b], lambda e, mv=mv, st=st: e.bn_aggr(out=mv[:, 0:2], in_=st[:].rearrange("p a b -> p (a b)")))
                        S.op('act', [mv_b, eps_ln_b], [mv_b], lambda e, mv=mv: e.activation(out=mv[:, 2:3], in_=mv[:, 1:2], func=AF.Sqrt, bias=eps_ln[:], scale=1.0))
                        S.op('dve', [mv_b], [mv_b], lambda e, mv=mv: e.reciprocal(out=mv[:, 2:3], in_=mv[:, 2:3]))
                        S.op('dve', [mv_b], [mv_b], lambda e, mv=mv: e.tensor_scalar(out=mv[:, 3:4], in0=mv[:, 0:1], scalar1=-1.0, scalar2=mv[:, 2:3], op0=ALU.mult, op1=ALU.mult))
                        S.op('act', [xt_b, mv_b], [zt_b], lambda e, zt=zt, xt=xt, mv=mv: e.activation(out=zt[:], in_=xt[:], func=AF.Identity, scale=mv[:, 2:3], bias=mv[:, 3:4]))
                        S.op('dve', [zt_b, lg_b], [zt_b], lambda e, zt=zt: e.tensor_tensor(out=zt[:], in0=zt[:], in1=lg[:], op=ALU.mult))
                        S.op('dve', [zt_b, lb_b], [zt_b], lambda e, zt=zt: e.tensor_tensor(out=zt[:], in0=zt[:], in1=lb[:], op=ALU.add))
                        S.dma('sp', out=xrows(xdst, tb), in_=zt[:], reads=[zt_b])
                        if ht is not None:
                            ht.emit(zt, zt_b, tb, nxt[0], nxt[1])
                    S.flush()

            tm_gemm_phase(w_out[l], M_d, KC, 512, NT, 512)
            ln_phase('in' if l == 0 else x2_d, x1_d, 0, (l, 1))

            with ExitStack() as es:
                slabs = [dict(j=j, segs=[(j * 256, 256, 0), (c.DFF + j * 256, 256, 256)]) for j in range(FC // 2)]
                pr = Ring([S.ps(es, 'pg', [128, 512]) for _ in range(8)])
                astr = Ring([S.sb(es, 'ast', [128, 2, 512], BF16) for _ in range(3)])
                sgr = Ring([S.sb(es, 'sg', [128, 512], F32) for _ in range(3)])

                def gu_compute(slab, tt, wb, wb_b, ab, ab_b):
                    j = slab['j']
                    ast, ast_b = astr.next()
                    pbs = []
                    for ci in range(4):
                        pb, pb_b = pr.next()
                        fm_group(pb, pb_b, wb, wb_b, ab, ab_b, KC, ci)
                        pbs.append((pb, pb_b))
                    for cc in range(2):
                        (pg, pg_b), (pu, pu_b) = pbs[cc], pbs[2 + cc]
                        sg, sg_b = sgr.next()
                        S.op('act', [pg_b], [sg_b], lambda e, sg=sg, pg=pg: e.activation(out=sg[:], in_=pg[:], func=AF.Silu))
                        S.op('dve', [sg_b, pu_b], [ast_b], lambda e, sg=sg, pu=pu, ast=ast, cc=cc: e.tensor_tensor(out=ast[:, cc, :], in0=pu[:], in1=sg[:], op=ALU.mult))
                    for hh in range(2):
                        S.dma('sp', out=A_d[tt * 2 + hh, :, j * 2:j * 2 + 2, :], in_=ast[:, :, hh * 256:(hh + 1) * 256], reads=[ast_b])

                run_gemm(es, w_gu[l], slabs, hT_d, KC, 512, NT, 512, gu_compute)
                S.flush()

            tm_gemm_phase(w_down[l], A_d, FC, 256, 2 * NT, 256)
            last = (l == DEPTH - 1)
            ln_phase(x1_d, 'out' if last else x2_d, 1, None if last else (l + 1, 0))

        S.flush(final=True)
    return nc, NV, blocks_na, var_tiles


_CACHE = {}


def host_constants(cfg, var_tiles, rpb_a):
    c = cfg
    NV = len(var_tiles)
    dr = np.stack([t[0] for t in var_tiles]); dc = np.stack([t[1] for t in var_tiles]); ok = np.stack([t[2] for t in var_tiles])
    g = rpb_a[:, :, dr, dc]
    rpbt = np.where(ok[None, None], g, np.float32(NEG)).astype(np.float32)
    W = max(c.S, 128)
    ropeB = np.zeros((3, 128, W), np.float32); ropeC = np.zeros((3, 128, W), np.float32)
    cb, sb_, pb = rope_tables(c.S, 'B'); cc, sc, pc = rope_tables(c.S, 'C')
    ropeB[0, :, :c.S] = cb; ropeB[1, :, :c.S] = sb_; ropeB[2, :, :128] = pb
    ropeC[0, :, :c.S] = cc; ropeC[1, :, :c.S] = sc; ropeC[2, :, :128] = pc
    k = np.arange(128)[:, None]; q = np.arange(128)[None, :]
    trim = np.stack([(k >= q), (k <= q)]).astype(np.float32)
    return dict(rpbt=rpbt, ropeB=ropeB, ropeC=ropeC, trim=trim, ident=np.eye(128, dtype=np.float32))


def run(cfg, inputs, n_cores):
    c = cfg
    key = (c.D, c.S, n_cores)
    if key not in _CACHE:
        _CACHE[key] = build(cfg)
    nc, NV, blocks_na, var_tiles = _CACHE[key]
    f = lambda a: np.ascontiguousarray(np.asarray(a, dtype=np.float32))
    consts = host_constants(cfg, var_tiles, f(inputs['rpb_a']))
    shared = dict(w_ada=f(inputs['w_ada']), b_ada=f(inputs['b_ada']), w_in=f(inputs['w_in']), w_branch=f(inputs['w_branch']),
                  w_out=f(inputs['w_out']), w_gate_up=f(inputs['w_gate_up']), w_down=f(inputs['w_down']),
                  ln_g=f(inputs['ln_g']), ln_b=f(inputs['ln_b']), lam_b=f(inputs['lam_b']).reshape(c.DEPTH, 256),
                  subln_b=f(inputs['subln_b']), sink_c=f(inputs['sink_c']), **consts)
    xs = f(inputs['x_sample']); xp = f(inputs['x_prompt'])
    cc_ = f(inputs['c']); cctx = f(inputs['c_ctx'])
    in_maps = []
    for i in range(n_cores):
        m = dict(shared)
        m['xs'] = xs[i]
        m['xp'] = xp[c.NP * i:c.NP * (i + 1)].reshape(c.NP * c.SEQ, c.D)
        m['cak'] = f(inputs['cache_a_k'])[i].reshape(c.DEPTH, c.PAST, c.MIXA)
        m['cav'] = f(inputs['cache_a_v'])[i].reshape(c.DEPTH, c.PAST, c.MIXA)
        m['cbk'] = f(inputs['cache_b_k'])[i].reshape(c.DEPTH, c.PAST, c.MIXB)
        m['cbv'] = f(inputs['cache_b_v'])[i].reshape(c.DEPTH, c.PAST, c.MIXB)
        m['cck'] = f(inputs['cache_c_k'])[i].reshape(c.DEPTH, c.PAST, c.KVW)
        m['ccv'] = f(inputs['cache_c_v'])[i].reshape(c.DEPTH, c.PAST, c.KVW)
        m['cvec'] = np.ascontiguousarray(np.stack([cc_[i], cctx]))
        in_maps.append(m)
    res = run_bass_kernel_spmd(nc, in_maps, core_ids=list(range(n_cores)))
    R = res.results
    cat = lambda k: np.concatenate([np.asarray(r[k]) for r in R], axis=0)
    y_prompt = cat('yp').reshape(n_cores * c.NP, c.SEQ, c.D)
    y_sample = np.stack([np.asarray(r['ys']) for r in R])
    nB = n_cores * c.NP
    outs = (y_prompt, y_sample,
            cat('nak').reshape(nB, c.DEPTH, c.SEQ, c.HA, 128), cat('nav').reshape(nB, c.DEPTH, c.SEQ, c.HA, 128),
            cat('nbk').reshape(nB, c.DEPTH, c.SEQ, c.HB, 2, 64), cat('nbv').reshape(nB, c.DEPTH, c.SEQ, c.HB, 128),
            cat('nck').reshape(nB, c.DEPTH, c.SEQ, c.KVC, 128), cat('ncv').reshape(nB, c.DEPTH, c.SEQ, c.KVC, 128))
    return tuple(np.ascontiguousarray(o, dtype=np.float32) for o in outs)


def kernel(**inputs):
    return run(Cfg(), inputs, 8)
```

```python
import math
import numpy as np
from contextlib import ExitStack
import concourse.bass as bass
import concourse.mybir as mybir
from concourse.bass_utils import run_bass_kernel_spmd

F32 = mybir.dt.float32
BF16 = mybir.dt.bfloat16
AF = mybir.ActivationFunctionType
ALU = mybir.AluOpType
AX = mybir.AxisListType
NEG = -30000.0
LN_EPS = 1e-5
RMS_EPS = 1e-5
ROPE_BASE = 10000.0


class Cfg:
    def __init__(s, D=4096, S=4096, SEQ=256, NP=2, PAST=512, DEPTH=2):
        s.D = D; s.S = S; s.SEQ = SEQ; s.NP = NP; s.PAST = PAST; s.DEPTH = DEPTH
        s.HA = D // 4 // 128; s.HB = s.HA; s.HC = D // 2 // 128; s.KVC = s.HC // 4; s.GC = 4
        s.MIXA = s.HA * 128; s.MIXB = s.HB * 128; s.MIXC = s.HC * 128; s.KVW = s.KVC * 128
        s.INW = 3 * s.MIXA + 3 * s.MIXB + s.MIXC + 2 * s.KVW + 3 * D
        s.DFF = ((8 * D // 3 + 255) // 256) * 256
        s.KC = D // 128; s.FC = s.DFF // 128
        s.NTS = S // 512; s.T = S + NP * SEQ; s.NT = s.T // 512
        s.R = S // 64; s.NB = S // 128; s.TB = s.T // 128
        s.VW = s.MIXA + s.MIXB + s.KVW
        s.ALPHA = (2 * DEPTH) ** 0.25
        s.PB = PAST // 128
        assert NP * SEQ == 512 and s.FC % 2 == 0 and S % 512 == 0
        s.CQA = 0; s.CKA = s.HA; s.CQB = 2 * s.HA; s.CKB = 2 * s.HA + s.HB; s.CKC = 2 * s.HA + 2 * s.HB
        s.NQK = s.CKC + s.KVC


def na_variants(R):
    NB = R // 2
    c = np.arange(64)
    ws = np.clip(c - 8, 0, 48)

    def tile_idx(i, m):
        dr = np.zeros((128, 128), np.int64); dc = np.zeros((128, 128), np.int64); ok = np.zeros((128, 128), bool)
        for krl in range(2):
            for rl in range(2):
                kr = 2 * m + krl; r = 2 * i + rl
                rs = min(max(r - 4, 0), R - 8)
                rowok = (kr >= rs) and (kr < rs + 8)
                kc = c[:, None]; qc = c[None, :]
                colok = (kc >= ws[None, :]) & (kc < ws[None, :] + 16)
                sl = (slice(krl * 64, krl * 64 + 64), slice(rl * 64, rl * 64 + 64))
                ok[sl] = colok & rowok
                dr[sl] = np.clip(kr - r + 7, 0, 14)
                dc[sl] = np.clip(kc - qc + 15, 0, 30)
        return dr, dc, ok

    sigs = {}; blocks = []; var_tiles = []
    for i in range(NB):
        rs0 = min(max(2 * i - 4, 0), R - 8); rs1 = min(max(2 * i + 1 - 4, 0), R - 8)
        m0 = min(rs0, rs1) // 2; m1 = (max(rs0, rs1) + 7) // 2
        ms = list(range(m0, m1 + 1))
        tiles = [tile_idx(i, m) for m in ms]
        key = tuple((t[0].tobytes(), t[1].tobytes(), t[2].tobytes()) for t in tiles)
        if key not in sigs:
            sigs[key] = len(var_tiles)
            var_tiles.extend(tiles)
        blocks.append((ms, sigs[key]))
    return blocks, var_tiles


def rope_tables(S, kind):
    pos = np.arange(S)
    row = (pos // 64).astype(np.float64); col = (pos % 64).astype(np.float64)
    cos = np.zeros((128, S), np.float32); sin = np.zeros((128, S), np.float32)
    perm = np.zeros((128, 128), np.float32)
    d = 128 if kind == 'C' else 64
    half, quarter = d // 2, d // 4
    inv = np.exp(-math.log(ROPE_BASE) * np.arange(quarter, dtype=np.float32) / quarter).astype(np.float32)
    for i in range(128):
        w = i % d
        h = w // half; ww = w % half; first = ww < quarter; j = ww % quarter
        p = row if h == 0 else col
        ang = (p.astype(np.float32) * inv[j]).astype(np.float32)
        cos[i] = np.cos(ang); sn = np.sin(ang)
        sin[i] = -sn if first else sn
        partner = i + quarter if first else i - quarter
        perm[partner, i] = 1.0
    return cos, sin, perm


class Buf:
    __slots__ = ('w', 'r', 'x')

    def __init__(s, x=False):
        s.w = None; s.r = {}; s.x = x


class Sched:
    ENG = ('pe', 'act', 'dve', 'pool', 'sp')
    KQ = 16

    def __init__(s, nc, es):
        s.nc = nc
        s.eobj = {'pe': nc.tensor, 'act': nc.scalar, 'dve': nc.vector, 'pool': nc.gpsimd, 'sp': nc.sync}
        s.sem = {e: es.enter_context(nc.semaphore('c_' + e)) for e in s.ENG}
        s.cnt = {e: 0 for e in s.ENG}
        s.kq = {'sp': 16, 'pool': 8, 'act': 8}
        s.ring = {q: [es.enter_context(nc.semaphore('r_%s%d' % (q, i))) for i in range(s.kq[q])] for q in ('sp', 'pool', 'act')}
        s.nd = {q: 0 for q in s.ring}
        s.ops = {e: [] for e in s.ENG}
        s.waited = {e: {} for e in s.ENG}
        s.semobj = {}
        s.bar = {}
        s.bar_pending = {e: False for e in s.ENG}
        s.nbuf = 0
        s.phase = 0
        s.dead = False
        import os as _os
        s.kstop = int(_os.environ.get('KSTOP', '0'))

    def sb(s, es, name, shape, dt):
        s.nbuf += 1
        t = es.enter_context(s.nc.sbuf_tensor('%s_%d' % (name, s.nbuf), list(shape), dt))
        return t, Buf()

    def ps(s, es, name, shape, dt=F32):
        s.nbuf += 1
        t = es.enter_context(s.nc.psum_tensor('%s_%d' % (name, s.nbuf), list(shape), dt))
        return t, Buf(True)

    def _tokwait(s, tok):
        if tok[0] == 'c':
            return s.sem[tok[1]], tok[2]
        return s.ring[tok[1]][tok[2]], tok[3]

    def _mk_waits(s, eng, toks):
        need = {}
        if s.bar_pending[eng]:
            s.bar_pending[eng] = False
            for k, (sem, v) in s.bar.items():
                need[k] = (sem, v)
        for tok in toks:
            if tok[0] == 'c' and tok[1] == 'pe' and eng == 'pe':
                continue
            sem, v = s._tokwait(tok)
            k = tok[:2] if tok[0] == 'c' else tok[:3]
            if k not in need or need[k][1] < v:
                need[k] = (sem, v)
        out = []
        wd = s.waited[eng]
        for k, (sem, v) in need.items():
            if wd.get(k, 0) >= v:
                continue
            wd[k] = v
            out.append((sem, v))
        return out

    def _deps(s, reads, writes, eng=None):
        toks = set()
        for b in reads:
            if b.w:
                toks.add(b.w)
            if b.x:
                for k, t in b.r.items():
                    if not (k[0] == 'c' and k[1] == eng):
                        toks.add(t)
        for b in writes:
            if b.w:
                toks.add(b.w)
            toks.update(b.r.values())
        return toks

    def _commit(s, tok, reads, writes):
        k = tok[:2] if tok[0] == 'c' else tok[:3]
        for b in reads:
            b.r[k] = tok
        for b in writes:
            b.w = tok; b.r = {}

    def op(s, eng, reads, writes, fn):
        if s.dead:
            return None
        toks = s._deps(reads, writes, eng)
        waits = s._mk_waits(eng, toks)
        s.cnt[eng] += 1
        tok = ('c', eng, s.cnt[eng])
        s.ops[eng].append((waits, fn, s.sem[eng], 1))
        s._commit(tok, reads, writes)
        return tok

    def dma(s, q, out, in_, reads=(), writes=(), **kw):
        if s.dead:
            return None
        toks = s._deps(reads, writes)
        KQ = s.kq[q]
        j = s.nd[q]; slot = j % KQ; tgt = 16 * (j // KQ + 1)
        if j >= KQ:
            toks.add(('d', q, slot, tgt - 16))
        waits = s._mk_waits(q, toks)
        s.nd[q] += 1
        tok = ('d', q, slot, tgt)
        s.ops[q].append((waits, (lambda e: e.dma_start(out=out, in_=in_, **kw)), s.ring[q][slot], 16))
        s._commit(tok, reads, writes)
        return tok

    def barrier(s):
        s.bar = {}
        for e in s.ENG:
            if s.cnt[e]:
                s.bar[('c', e)] = (s.sem[e], s.cnt[e])
        for q in s.ring:
            for slot in range(s.kq[q]):
                n = (s.nd[q] - slot + s.kq[q] - 1) // s.kq[q] if s.nd[q] > slot else 0
                if n:
                    s.bar[('d', q, slot)] = (s.ring[q][slot], 16 * n)
        for e in s.ENG:
            s.bar_pending[e] = True

    def flush(s, final=False):
        if s.dead:
            return
        s.phase += 1
        print("phase", s.phase, {e: len(v) for e, v in s.ops.items()}, flush=True)
        if s.kstop and s.phase == s.kstop:
            final = True
            s.dead = True
        s.barrier()
        if final:
            for e in s.ENG:
                w = s._mk_waits(e, [])
                s.ops[e].append((w, None, None, 0))
        ops = s.ops
        s.ops = {e: [] for e in s.ENG}
        with s.nc.Block() as block:
            def mk(elist):
                def body(eng):
                    for waits, fn, sem, inc in elist:
                        for (ws, wv) in waits:
                            eng.wait_ge(ws, wv)
                        if fn is not None:
                            ins = fn(eng)
                            ins.then_inc(sem, inc)
                return body
            if ops['sp']:
                block.sync(mk(ops['sp']))
            if ops['pool']:
                block.gpsimd(mk(ops['pool']))
            if ops['act']:
                block.scalar(mk(ops['act']))
            if ops['dve']:
                block.vector(mk(ops['dve']))
            if ops['pe']:
                block.tensor(mk(ops['pe']))


class Ring:
    def __init__(s, items):
        s.items = items; s.i = 0

    def next(s):
        it = s.items[s.i % len(s.items)]; s.i += 1
        return it


class Pipe:
    def __init__(s, depth):
        s.q = []; s.depth = depth; s.late = []

    def tick(s):
        while len(s.q) > s.depth - 1:
            s.q.pop(0)()
        for it in s.late:
            it[0] -= 1
        while s.late and s.late[0][0] <= 0:
            s.late.pop(0)[1]()

    def push(s, fn):
        s.q.append(fn)

    def push_late(s, n, fn):
        s.late.append([n, fn])

    def flush(s):
        while s.q or s.late:
            while s.q:
                s.q.pop(0)()
            if s.late:
                s.late.pop(0)[1]()


def build(cfg):
    c = cfg
    D, S_, T, KC, FC, NT, NTS, DEPTH = c.D, c.S, c.T, c.KC, c.FC, c.NT, c.NTS, c.DEPTH
    nc = bass.Bass("TRN2", target_bir_lowering=False)

    def din(name, shape, dt=F32):
        return nc.dram_tensor(name, list(shape), dt, kind="ExternalInput").ap()

    def dout(name, shape):
        return nc.dram_tensor(name, list(shape), F32, kind="ExternalOutput").ap()

    def dscr(name, shape, dt):
        return nc.dram_tensor(name, list(shape), dt, kind="Internal").ap()

    xs = din("xs", [S_, D]); xp = din("xp", [512, D])
    cak = din("cak", [DEPTH, c.PAST, c.MIXA]); cav = din("cav", [DEPTH, c.PAST, c.MIXA])
    cbk = din("cbk", [DEPTH, c.PAST, c.MIXB]); cbv = din("cbv", [DEPTH, c.PAST, c.MIXB])
    cck = din("cck", [DEPTH, c.PAST, c.KVW]); ccv = din("ccv", [DEPTH, c.PAST, c.KVW])
    cvec = din("cvec", [2, D])
    w_ada = din("w_ada", [DEPTH, D, 6 * D]); b_ada = din("b_ada", [DEPTH, 6 * D])
    w_in = din("w_in", [DEPTH, D, c.INW]); w_branch = din("w_branch", [DEPTH, D, D])
    w_out = din("w_out", [DEPTH, D, D]); w_gu = din("w_gate_up", [DEPTH, D, 2 * c.DFF])
    w_down = din("w_down", [DEPTH, c.DFF, D])
    ln_g = din("ln_g", [DEPTH, 2, D]); ln_b = din("ln_b", [DEPTH, 2, D])
    lam_b = din("lam_b", [DEPTH, 256]); subln = din("subln_b", [DEPTH, 128]); sink = din("sink_c", [DEPTH, c.HC])
    blocks_na, var_tiles = na_variants(c.R)
    NV = len(var_tiles)
    rpbt = din("rpbt", [DEPTH, c.HA, NV, 128, 128])
    ident_d = din("ident", [128, 128])
    ropeB = din("ropeB", [3, 128, max(S_, 128)]); ropeC = din("ropeC", [3, 128, max(S_, 128)])
    trim = din("trim", [2, 128, 128])

    ys = dout("ys", [S_, D]); yp = dout("yp", [512, D])
    nak = dout("nak", [c.NP, DEPTH, c.SEQ, c.MIXA]); nav = dout("nav", [c.NP, DEPTH, c.SEQ, c.MIXA])
    nbk = dout("nbk", [c.NP, DEPTH, c.SEQ, c.MIXB]); nbv = dout("nbv", [c.NP, DEPTH, c.SEQ, c.MIXB])
    nck = dout("nck", [c.NP, DEPTH, c.SEQ, c.KVW]); ncv = dout("ncv", [c.NP, DEPTH, c.SEQ, c.KVW])

    mod_d = dscr("mod_d", [DEPTH, 2, 6 * D], F32)
    hT_d = dscr("hT_d", [NT, 128, KC, 512], BF16)
    QK_d = dscr("QK_d", [c.NQK, 128, T], BF16)
    QC_d = dscr("QC_d", [c.KVC, 128, c.TB, 4, 128], BF16)
    V_d = dscr("V_d", [c.TB, 128, c.VW], BF16)
    G_d = dscr("G_d", [NT, D // 512, 128, 3, 4, 512], BF16)
    O_d = dscr("O_d", [NT, 128, KC, 512], BF16)
    M_d = dscr("M_d", [NT, 128, KC, 512], BF16)
    A_d = dscr("A_d", [2 * NT, 128, FC, 256], BF16)
    y_d = dscr("y_d", [T, D], F32)
    x1_d = dscr("x1_d", [T, D], F32)
    x2_d = dscr("x2_d", [T, D], F32)

    def xrows(src, tb):
        if src == 'in':
            nsb = S_ // 128
            return xs[tb * 128:(tb + 1) * 128, :] if tb < nsb else xp[(tb - nsb) * 128:(tb - nsb + 1) * 128, :]
        if src == 'out':
            nsb = S_ // 128
            return ys[tb * 128:(tb + 1) * 128, :] if tb < nsb else yp[(tb - nsb) * 128:(tb - nsb + 1) * 128, :]
        return src[tb * 128:(tb + 1) * 128, :]

    import os as _osq
    LNQ = _osq.environ.get('LNQ', 'pool')
    ATQ = _osq.environ.get('ATQ', 'pool')
    KMIX = _osq.environ.get('KMIX', 'ABC')
    with ExitStack() as top:
        S = Sched(nc, top)
        ident, ident_b = S.sb(top, 'ident', [128, 128], F32)
        identb, identb_b = S.sb(top, 'identb', [128, 128], BF16)
        ones_bf, ones_bf_b = S.sb(top, 'ones_bf', [128, 128], BF16)
        ones_f, ones_f_b = S.sb(top, 'ones_f', [128, 128], F32)
        eps_ln, eps_ln_b = S.sb(top, 'eps_ln', [128, 1], F32)
        eps_rms, eps_rms_b = S.sb(top, 'eps_rms', [128, 1], F32)
        modT = []
        for l in range(DEPTH):
            modT.append(S.sb(top, 'modT', [128, 6, KC, 2], F32))
        evac_flip = [0]

        def copy_evac(out_ap, in_ap, reads, writes):
            evac_flip[0] ^= 1
            import os as _os2
            _m = _os2.environ.get('KDBG', '')
            if (evac_flip[0] and 'dveonly' not in _m) or 'actonly' in _m:
                S.op('act', reads, writes, lambda e: e.activation(out=out_ap, in_=in_ap, func=AF.Copy))
            else:
                S.op('dve', reads, writes, lambda e: e.tensor_copy(out=out_ap, in_=in_ap))

        with ExitStack() as es:
            S.dma('sp', out=ident[:], in_=ident_d[:, :], writes=[ident_b])
            S.op('dve', [ident_b], [identb_b], lambda e: e.tensor_copy(out=identb[:], in_=ident[:]))
            S.op('dve', [], [ones_bf_b], lambda e: e.memset(ones_bf[:], 1.0))
            S.op('dve', [], [ones_f_b], lambda e: e.memset(ones_f[:], 1.0))
            S.op('dve', [], [eps_ln_b], lambda e: e.memset(eps_ln[:], LN_EPS))
            S.op('dve', [], [eps_rms_b], lambda e: e.memset(eps_rms[:], RMS_EPS))
            cT, cT_b = S.sb(es, 'cT', [128, KC, 2], F32)
            scT, scT_b = S.sb(es, 'scT', [128, KC, 2], F32)
            for g in range(2):
                S.dma('sp', out=cT[:, :, g:g + 1], in_=cvec[g:g + 1, :].rearrange("g (k p) -> p k g", p=128),
                      writes=[cT_b], allow_slow_non_contiguous=True)
            S.op('act', [cT_b], [scT_b], lambda e: e.activation(out=scT[:], in_=cT[:], func=AF.Silu))
            CB = 2048
            wring = Ring([S.sb(es, 'wada', [128, CB], F32) for _ in range(8)])
            pring = Ring([S.ps(es, 'pada', [2, CB]) for _ in range(2)])
            bring = Ring([S.sb(es, 'bada', [2, CB], F32) for _ in range(2)])
            mring = Ring([S.sb(es, 'mrow', [2, CB], F32) for _ in range(2)])
            for l in range(DEPTH):
                for cb in range(6 * D // CB):
                    pt, pt_b = pring.next()
                    bt, bt_b = bring.next()
                    for g in range(2):
                        S.dma('sp', out=bt[g:g + 1, :], in_=b_ada[l:l + 1, cb * CB:(cb + 1) * CB], writes=[bt_b])
                    for k in range(KC):
                        wt, wt_b = wring.next()
                        S.dma('sp', out=wt[:], in_=w_ada[l, k * 128:(k + 1) * 128, cb * CB:(cb + 1) * CB], writes=[wt_b])

                        def mm(e, wt=wt, pt=pt, k=k):
                            for j in range(CB // 512):
                                ins = e.matmul(pt[:, j * 512:(j + 1) * 512], lhsT=scT[:, k, :], rhs=wt[:, j * 512:(j + 1) * 512],
                                               start=(k == 0), stop=(k == KC - 1))
                            return ins
                        S.op('pe', [scT_b, wt_b], [pt_b], mm)
                    mr, mr_b = mring.next()
                    S.op('dve', [pt_b, bt_b], [mr_b], lambda e, mr=mr, pt=pt, bt=bt: e.tensor_tensor(out=mr[:], in0=pt[:], in1=bt[:], op=ALU.add))
                    S.dma('sp', out=mod_d[l, :, cb * CB:(cb + 1) * CB], in_=mr[:], reads=[mr_b])
            S.flush()

        with ExitStack() as es:
            for l in range(DEPTH):
                mt, mt_b = modT[l]
                for i in range(6):
                    for g in range(2):
                        S.dma('sp', out=mt[:, i, :, g:g + 1],
                              in_=mod_d[l, g:g + 1, i * D:(i + 1) * D].rearrange("g (k p) -> p k g", p=128),
                              writes=[mt_b], allow_slow_non_contiguous=True)
                for i in (1, 4):
                    S.op('dve', [mt_b], [mt_b], lambda e, mt=mt, i=i: e.tensor_scalar(out=mt[:, i, :, :], in0=mt[:, i, :, :], scalar1=1.0, scalar2=None, op0=ALU.add))
            S.flush()

        class HT:
            def __init__(h, es, store_q=None, act_only=False):
                store_q = store_q or LNQ
                h.pring = Ring([S.ps(es, 'ptr', [128, 4, 128]) for _ in range(3)])
                h.hst, h.hst_b = S.sb(es, 'hst', [128, KC, 512], BF16)
                h.flip = 0
                h.store_q = store_q; h.act_only = act_only

            def emit(h, xt, xt_b, tb, l, i):
                g = 0 if tb < S_ // 128 else 1
                mt, mt_b = modT[l]
                sub = tb % 4
                for cg in range(KC // 4):
                    pt, pt_b = h.pring.next()

                    def tr(e, pt=pt, cg=cg):
                        for j in range(4):
                            ins = e.transpose(out=pt[:, j, :], in_=xt[:, (cg * 4 + j) * 128:(cg * 4 + j + 1) * 128], identity=ident[:])
                        return ins
                    S.op('pe', [xt_b, ident_b], [pt_b], tr)
                    for j in range(4):
                        k = cg * 4 + j
                        o = h.hst[:, k, sub * 128:(sub + 1) * 128]
                        sc = mt[:, 3 * i + 1, k, g:g + 1]; sh = mt[:, 3 * i, k, g:g + 1]
                        h.flip ^= 1
                        if h.flip or h.act_only:
                            S.op('act', [pt_b, mt_b], [h.hst_b], lambda e, o=o, pt=pt, j=j, sc=sc, sh=sh:
                                 e.activation(out=o, in_=pt[:, j, :], func=AF.Identity, scale=sc, bias=sh))
                        else:
                            S.op('dve', [pt_b, mt_b], [h.hst_b], lambda e, o=o, pt=pt, j=j, sc=sc, sh=sh:
                                 e.tensor_scalar(out=o, in0=pt[:, j, :], scalar1=sc, scalar2=sh, op0=ALU.mult, op1=ALU.add))
                if sub == 3:
                    S.dma(h.store_q, out=hT_d[tb // 4], in_=h.hst[:], reads=[h.hst_b])

        with ExitStack() as es:
            ht = HT(es)
            xring = Ring([S.sb(es, 'xt', [128, D], F32) for _ in range(3)])
            for tb in range(c.TB):
                xt, xt_b = xring.next()
                S.dma('sp', out=xt[:], in_=xrows('in', tb), writes=[xt_b])
                ht.emit(xt, xt_b, tb, 0, 0)
            S.flush()

        def run_gemm(es, W, slabs, act_d, KCg, TT, NTT, NW, compute, kgroup=4):
            wr = [S.sb(es, 'wslab', [128, KCg, NW], BF16) for _ in range(2)]
            ar = [S.sb(es, 'atile', [128, KCg, TT], BF16) for _ in range(2)]
            items = [(si, tt) for si in range(len(slabs)) for tt in range(NTT)]

            def load_w(si):
                wb, wb_b = wr[si % 2]
                for k0 in range(0, KCg, kgroup):
                    kn = min(kgroup, KCg - k0)
                    for (c0, w, d0) in slabs[si]['segs']:
                        S.dma('pool', out=wb[:, k0:k0 + kn, d0:d0 + w],
                              in_=W[k0 * 128:(k0 + kn) * 128, c0:c0 + w].rearrange("(c p) n -> p c n", p=128), writes=[wb_b])

            def load_a(n):
                si, tt = items[n]
                ab, ab_b = ar[n % 2]
                S.dma('sp', out=ab[:], in_=act_d[tt], writes=[ab_b])

            load_w(0); load_a(0)
            for n, (si, tt) in enumerate(items):
                if tt == 0 and si + 1 < len(slabs):
                    load_w(si + 1)
                if n + 1 < len(items):
                    load_a(n + 1)
                wb, wb_b = wr[si % 2]; ab, ab_b = ar[n % 2]
                compute(slabs[si], tt, wb, wb_b, ab, ab_b)

        def fm_group(pb, pb_b, wb, wb_b, ab, ab_b, KCg, ci, ks=None, ncols=512):
            ks = list(range(KCg)) if ks is None else ks

            def mm(e):
                for n_, k in enumerate(ks):
                    ins = e.matmul(pb[:, 0:ncols], lhsT=wb[:, k, ci * 128:(ci + 1) * 128], rhs=ab[:, k, 0:ncols],
                                   start=(n_ == 0), stop=(n_ == len(ks) - 1))
                return ins
            S.op('pe', [wb_b, ab_b], [pb_b], mm)

        def tm_group(pb, pb_b, wb, wb_b, ab, ab_b, KCg, sub, w):
            def mm(e):
                for k in range(KCg):
                    ins = e.matmul(pb[:, 0:w], lhsT=ab[:, k, sub * 128:(sub + 1) * 128], rhs=wb[:, k, 0:w],
                                   start=(k == 0), stop=(k == KCg - 1))
                return ins
            S.op('pe', [wb_b, ab_b], [pb_b], mm)

        for l in range(DEPTH):
            lam_init = 0.8 - 0.6 * math.exp(-0.3 * l)
            import os as _os
            with ExitStack() as es:
                secs = [('qa', c.MIXA), ('ka', c.MIXA), ('va', c.MIXA), ('qb', c.MIXB), ('kb', c.MIXB), ('vb', c.MIXB),
                        ('qc', c.MIXC), ('kc', c.KVW), ('vc', c.KVW), ('g0', D), ('g1', D), ('g2', D)]
                slabs = []
                col = 0
                for (nm, wd) in secs:
                    for off in range(0, wd, 512):
                        w = min(512, wd - off)
                        slabs.append(dict(sec=nm, off=off, w=w, segs=[(col + off, w, 0)]))
                    col += wd
                pring = Ring([S.ps(es, 'pq', [128, 512]) for _ in range(5)])
                pring2 = Ring([S.ps(es, 'pq2', [128, 512]) for _ in range(2)])
                ostr = Ring([S.sb(es, 'ost', [128, 4, 512], BF16) for _ in range(3)])
                fstr = Ring([S.sb(es, 'fst', [128, 4, 512], F32) for _ in range(2)])
                xbr = Ring([S.sb(es, 'xb', [128, 512], BF16) for _ in range(3)])
                t1r = Ring([S.sb(es, 't1', [128, 512], F32) for _ in range(3)])
                t2r = Ring([S.sb(es, 't2', [128, 512], F32) for _ in range(3)])
                cosr = Ring([S.sb(es, 'cos', [128, 512], F32) for _ in range(2)])
                sinr = Ring([S.sb(es, 'sin', [128, 512], F32) for _ in range(2)])
                permB, permB_b = S.sb(es, 'permB', [128, 128], BF16)
                permC, permC_b = S.sb(es, 'permC', [128, 128], BF16)
                S.dma('pool', out=permB[:], in_=ropeB[2, :, 0:128], writes=[permB_b])
                S.dma('pool', out=permC[:], in_=ropeC[2, :, 0:128], writes=[permC_b])
                kout = {'ka': (nak, c.CKA), 'kb': (nbk, c.CKB), 'kc': (nck, c.CKC)}
                qkbase = {'qa': c.CQA, 'ka': c.CKA, 'qb': c.CQB, 'kb': c.CKB, 'kc': c.CKC}
                vout = {'va': (nav, 0), 'vb': (nbv, c.MIXA), 'vc': (ncv, c.MIXA + c.MIXB)}

                def store_rows(dst, off, w, fst, fst_b):
                    for b in range(c.NP):
                        S.dma('sp', out=dst[b, l, :, off:off + w].rearrange("(h p) w -> p h w", p=128),
                              in_=fst[:, 2 * b:2 * b + 2, 0:w], reads=[fst_b])

                def qkv_compute(slab, tt, wb, wb_b, ab, ab_b):
                    sec, off, w = slab['sec'], slab['off'], slab['w']
                    nch = w // 128
                    if sec in vout:
                        dst, voff = vout[sec]
                        ost, ost_b = ostr.next()
                        fst = fst_b = None
                        if tt == NT - 1:
                            fst, fst_b = fstr.next()
                        for sub in range(4):
                            pb, pb_b = pring.next()
                            tm_group(pb, pb_b, wb, wb_b, ab, ab_b, KC, sub, w)
                            copy_evac(ost[:, sub, 0:w], pb[:, 0:w], [pb_b], [ost_b])
                            if fst is not None:
                                copy_evac(fst[:, sub, 0:w], pb[:, 0:w], [pb_b], [fst_b])
                        if 'novd' not in _os.environ.get('KDBG', ''):
                            S.dma('sp', out=V_d[tt * 4:(tt + 1) * 4, :, voff + off:voff + off + w].rearrange("s p w -> p s w"),
                                  in_=ost[:, :, 0:w], reads=[ost_b])
                        if fst is not None and 'noout' not in _os.environ.get('KDBG', ''):
                            store_rows(dst, off, w, fst, fst_b)
                        return
                    rope = sec in ('qb', 'kb', 'qc', 'kc') and tt < NTS
                    if rope:
                        tab = ropeB if sec in ('qb', 'kb') else ropeC
                        perm, perm_b = (permB, permB_b) if sec in ('qb', 'kb') else (permC, permC_b)
                        ct, ct_b = cosr.next(); st, st_b = sinr.next()
                        S.dma('sp', out=ct[:], in_=tab[0, :, tt * 512:(tt + 1) * 512], writes=[ct_b])
                        S.dma('sp', out=st[:], in_=tab[1, :, tt * 512:(tt + 1) * 512], writes=[st_b])
                    ost, ost_b = ostr.next()
                    pending = []
                    for ci in range(nch):
                        pb, pb_b = pring.next()
                        fm_group(pb, pb_b, wb, wb_b, ab, ab_b, KC, ci)
                        while pending:
                            pending.pop()()
                        if sec[0] == 'g':
                            S.op('act', [pb_b], [ost_b], lambda e, o=ost[:, ci, :], pb=pb: e.activation(out=o, in_=pb[:], func=AF.Sigmoid))
                        elif rope:
                            xb, xb_b = xbr.next(); t1, t1_b = t1r.next(); t2, t2_b = t2r.next()
                            S.op('act', [pb_b], [xb_b], lambda e, xb=xb, pb=pb: e.activation(out=xb[:], in_=pb[:], func=AF.Copy))
                            S.op('dve', [pb_b, ct_b], [t1_b], lambda e, t1=t1, pb=pb, ct=ct: e.tensor_tensor(out=t1[:], in0=pb[:], in1=ct[:], op=ALU.mult))

                            def later(xb=xb, xb_b=xb_b, t1=t1, t1_b=t1_b, t2=t2, t2_b=t2_b, ci=ci, st=st, st_b=st_b, perm=perm, perm_b=perm_b):
                                p2, p2_b = pring2.next()
                                S.op('pe', [xb_b, perm_b], [p2_b], lambda e: e.matmul(p2[:], lhsT=perm[:], rhs=xb[:], start=True, stop=True))
                                S.op('dve', [p2_b, st_b], [t2_b], lambda e: e.tensor_tensor(out=t2[:], in0=p2[:], in1=st[:], op=ALU.mult))
                                S.op('dve', [t1_b, t2_b], [ost_b], lambda e: e.tensor_tensor(out=ost[:, ci, :], in0=t1[:], in1=t2[:], op=ALU.add))
                            pending.append(later)
                        else:
                            copy_evac(ost[:, ci, :], pb[:], [pb_b], [ost_b])
                    while pending:
                        pending.pop()()
                    if sec[0] == 'g':
                        br = int(sec[1]); fs = off // 512
                        S.dma('sp', out=G_d[tt, fs, :, br, :, :], in_=ost[:], reads=[ost_b])
                    elif sec == 'qc':
                        n = off // 512
                        for j in range(4):
                            S.dma('sp', out=QC_d[n, :, tt * 4 + j, :, :], in_=ost[:, :, j * 128:(j + 1) * 128], reads=[ost_b])
                    else:
                        c0 = qkbase[sec] + off // 128
                        S.dma('sp', out=QK_d[c0:c0 + nch, :, tt * 512:(tt + 1) * 512].rearrange("c p t -> p c t"),
                              in_=ost[:, 0:nch, :], reads=[ost_b])
                    if sec in kout and tt == NT - 1:
                        dst, _ = kout[sec]
                        fst, fst_b = fstr.next()
                        for sub in range(4):
                            pb, pb_b = pring.next()
                            tm_group(pb, pb_b, wb, wb_b, ab, ab_b, KC, sub, w)
                            copy_evac(fst[:, sub, 0:w], pb[:, 0:w], [pb_b], [fst_b])
                        store_rows(dst, off, w, fst, fst_b)

                import os as _os
                _ks = _os.environ.get('KSLAB')
                if _ks:
                    slabs = [slabs[int(x)] for x in _ks.split(',')]
                run_gemm(es, w_in[l], slabs, hT_d, KC, 512, NT, 512, qkv_compute)
                S.flush()

            with ExitStack() as es:
                HAL = c.HA + c.HB + c.KVC
                ckT, ckT_b = S.sb(es, 'ckT', [128, HAL, c.PAST], BF16)
                cvt, cvt_b = S.sb(es, 'cvt', [128, c.PB, c.VW], BF16)
                for (src, off, w) in ((cav, 0, c.MIXA), (cbv, c.MIXA, c.MIXB), (ccv, c.MIXA + c.MIXB, c.KVW)):
                    for w0 in range(0, w, 512):
                        ww = min(512, w - w0)
                        S.dma('pool', out=cvt[:, :, off + w0:off + w0 + ww],
                              in_=src[l, :, w0:w0 + ww].rearrange("(j p) w -> p j w", p=128), writes=[cvt_b])
                pSp = [es.enter_context(nc.psum_tensor('pSp%d_%d' % (l, i), [128, 2, 512], F32)) for i in range(2)]
                pS = Ring([(pSp[i][:, j, :], Buf(True)) for i in range(2) for j in range(2)])
                pO = [S.ps(es, 'pO', [128, 512]) for _ in range(4)]
                ckf_r = Ring([S.sb(es, 'ckf', [128, 1024], F32) for _ in range(2)])
                hbase = 0
                for (src, nh) in ((cak, c.HA), (cbk, c.HB), (cck, c.KVC)):
                    for j in range(c.PB):
                        for h0 in range(0, nh, 8):
                            hn = min(8, nh - h0)
                            ckf, ckf_b = ckf_r.next()
                            S.dma('sp', out=ckf[:, 0:hn * 128], in_=src[l, j * 128:(j + 1) * 128, h0 * 128:(h0 + hn) * 128], writes=[ckf_b])
                            for h1 in range(0, hn, 4):
                                h2 = min(4, hn - h1)
                                pt, pt_b = pS.next()

                                def tr(e, pt=pt, ckf=ckf, h1=h1, h2=h2):
                                    for jj in range(h2):
                                        ins = e.transpose(out=pt[:, jj * 128:(jj + 1) * 128], in_=ckf[:, (h1 + jj) * 128:(h1 + jj + 1) * 128], identity=ident[:])
                                    return ins
                                S.op('pe', [ckf_b, ident_b], [pt_b], tr)
                                hh = hbase + h0 + h1
                                copy_evac(ckT[:, hh:hh + h2, j * 128:(j + 1) * 128],
                                          pt[:, 0:h2 * 128].rearrange("p (a b) -> p a b", b=128), [pt_b], [ckT_b])
                    hbase += nh
                lamt, lamt_b = S.sb(es, 'lamt', [128, 256], F32)
                lsc, lsc_b = S.sb(es, 'lsc', [128, 8], F32)
                gainc, gainc_b = S.sb(es, 'gainc', [128, 1], F32)
                esink, esink_b = S.sb(es, 'esink', [128, c.HC], F32)
                S.dma('sp', out=lamt[:], in_=lam_b[l:l + 1, :].to_broadcast([128, 256]), writes=[lamt_b])
                S.dma('sp', out=esink[:], in_=sink[l:l + 1, :].to_broadcast([128, c.HC]), writes=[esink_b])
                S.dma('sp', out=gainc[:], in_=subln[l:l + 1, :].rearrange("o p -> p o"), writes=[gainc_b], allow_slow_non_contiguous=True)
                S.op('act', [esink_b], [esink_b], lambda e: e.activation(out=esink[:], in_=esink[:], func=AF.Exp))
                S.op('dve', [gainc_b], [gainc_b], lambda e: e.tensor_scalar(out=gainc[:], in0=gainc[:], scalar1=(1.0 - lam_init), scalar2=None, op0=ALU.mult))
                S.op('dve', [lamt_b], [lamt_b], lambda e: e.tensor_tensor(out=lamt[:, 0:64], in0=lamt[:, 0:64], in1=lamt[:, 64:128], op=ALU.mult))
                S.op('dve', [lamt_b], [lamt_b], lambda e: e.tensor_tensor(out=lamt[:, 128:192], in0=lamt[:, 128:192], in1=lamt[:, 192:256], op=ALU.mult))
                S.op('dve', [lamt_b], [lsc_b], lambda e: e.tensor_reduce(out=lsc[:, 0:1], in_=lamt[:, 0:64], axis=AX.X, op=ALU.add))
                S.op('dve', [lamt_b, lsc_b], [lsc_b], lambda e: e.tensor_reduce(out=lsc[:, 1:2], in_=lamt[:, 128:192], axis=AX.X, op=ALU.add))
                S.op('act', [lsc_b], [lsc_b], lambda e: e.activation(out=lsc[:, 2:4], in_=lsc[:, 0:2], func=AF.Exp))
                S.op('dve', [lsc_b], [lsc_b], lambda e: e.tensor_tensor(out=lsc[:, 4:5], in0=lsc[:, 3:4], in1=lsc[:, 2:3], op=ALU.subtract))
                S.op('dve', [lsc_b], [lsc_b], lambda e: e.tensor_scalar(out=lsc[:, 5:6], in0=lsc[:, 4:5], scalar1=-lam_init, scalar2=None, op0=ALU.add))
                neglam = lsc[:, 5:6]
                trimt, trimt_b = S.sb(es, 'trimt', [128, 2, 128], BF16)
                S.dma('pool', out=trimt[:], in_=trim.rearrange("a p q -> p a q"), writes=[trimt_b])

                pTr = Ring([S.sb(es, 'pT', [128, 512], BF16) for _ in range(6)])
                rdr = Ring([S.sb(es, 'rd', [128, 512], F32) for _ in range(3)])
                f1r = Ring([S.sb(es, 'f1', [128, 512], F32) for _ in range(3)])
                f2r = Ring([S.sb(es, 'f2', [128, 512], F32) for _ in range(3)])

                def load_kv(kt, kt_b, vt, vt_b, chunk, voff, t0, nblk):
                    S.dma('sp', out=kt[:, 0:nblk * 128], in_=QK_d[chunk, :, t0:t0 + nblk * 128], writes=[kt_b])
                    S.dma('sp', out=vt[:, 0:nblk, :], in_=V_d[t0 // 128:t0 // 128 + nblk, :, voff:voff + 128].rearrange("b p w -> p b w"),
                          writes=[vt_b])

                PP = Pipe(2)
                pset = [0]

                def attn_single(groups, nq, qaps, scale, extra_den, out_ap, out_b, kbufs, oshp=(lambda a: a), after=None):
                    pset[0] ^= 1
                    (po, po_b), (pd, pd_b) = pO[2 * pset[0]], pO[2 * pset[0] + 1]
                    nblk_total = sum(len(g[0]) for g in groups)
                    seen = [0]

                    def pv_stage(blocks, pt, pt_b):
                        def mm(e):
                            for bi, (kT_ap, v_ap) in enumerate(blocks):
                                first = (seen[0] == 0); seen[0] += 1; last = (seen[0] == nblk_total)
                                e.matmul(po[:, 0:nq], lhsT=v_ap, rhs=pt[:, bi * nq:(bi + 1) * nq], start=first, stop=last)
                                ins = e.matmul(pd[:, 0:nq], lhsT=ones_bf[:], rhs=pt[:, bi * nq:(bi + 1) * nq], start=first, stop=last)
                            return ins
                        S.op('pe', [pt_b, ones_bf_b] + kbufs, [po_b, pd_b], mm)

                    for (blocks, post) in groups:
                        ps_, ps_b = pS.next()
                        ncol = len(blocks) * nq

                        def qk(e, blocks=blocks, ps_=ps_):
                            for bi, (kT_ap, v_ap) in enumerate(blocks):
                                ins = e.matmul(ps_[:, bi * nq:(bi + 1) * nq], lhsT=kT_ap, rhs=qaps, start=True, stop=True)
                            return ins
                        S.op('pe', kbufs, [ps_b], qk)
                        PP.tick()
                        pt, pt_b = pTr.next()
                        S.op('act', [ps_b], [pt_b], lambda e, pt=pt, ps_=ps_, ncol=ncol: e.activation(out=pt[:, 0:ncol], in_=ps_[:, 0:ncol], func=AF.Exp, scale=scale))
                        if post is not None:
                            post_ap, post_b = post
                            S.op('dve', [pt_b, post_b], [pt_b], lambda e, pt=pt, ncol=ncol, post_ap=post_ap: e.tensor_tensor(
                                out=post_ap[0](pt[:, 0:ncol]), in0=post_ap[0](pt[:, 0:ncol]), in1=post_ap[1], op=ALU.mult))
                        PP.push(lambda blocks=blocks, pt=pt, pt_b=pt_b: pv_stage(blocks, pt, pt_b))

                    def epilogue():
                        rd, rd_b = rdr.next()
                        if extra_den is not None:
                            ed_ap, ed_b, shp = extra_den
                            S.op('dve', [pd_b, ed_b], [rd_b], lambda e: e.tensor_tensor(out=shp(rd[:, 0:nq]), in0=shp(pd[:, 0:nq]), in1=ed_ap, op=ALU.add))
                            S.op('dve', [rd_b], [rd_b], lambda e: e.reciprocal(out=rd[:, 0:nq], in_=rd[:, 0:nq]))
                        else:
                            S.op('dve', [pd_b], [rd_b], lambda e: e.reciprocal(out=rd[:, 0:nq], in_=pd[:, 0:nq]))
                        S.op('dve', [po_b, rd_b], [out_b], lambda e: e.tensor_tensor(out=out_ap, in0=oshp(po[:, 0:nq]), in1=oshp(rd[:, 0:nq]), op=ALU.mult))
                        if after is not None:
                            after()
                    PP.push(epilogue)

                def attn_diff(blocks, nq, qt, scale, out_ap, out_b, kbufs, after=None):
                    assert nq <= 256
                    pset[0] ^= 1
                    (oa, oa_b), (ob, ob_b) = pO[2 * pset[0]], pO[2 * pset[0] + 1]
                    nb = len(blocks)

                    def pv_stage(bi, v_ap, p, p_b):
                        def mm(e):
                            e.matmul(oa[:, 0:nq], lhsT=v_ap, rhs=p[:, 0:nq], start=(bi == 0), stop=False)
                            e.matmul(oa[:, 256:256 + nq], lhsT=ones_bf[:], rhs=p[:, 0:nq], start=False, stop=(bi == nb - 1))
                            e.matmul(ob[:, 0:nq], lhsT=v_ap, rhs=p[:, nq:2 * nq], start=(bi == 0), stop=False)
                            return e.matmul(ob[:, 256:256 + nq], lhsT=ones_bf[:], rhs=p[:, nq:2 * nq], start=False, stop=(bi == nb - 1))
                        S.op('pe', [p_b, ones_bf_b] + kbufs, [oa_b, ob_b], mm)

                    for bi, (kT_ap, v_ap) in enumerate(blocks):
                        if pS.i % 2:
                            pS.next()
                        pair = pSp[(pS.i % 4) // 2]
                        s1, s1_b = pS.next(); s2, s2_b = pS.next()

                        def qk(e, kT_ap=kT_ap, s1=s1, s2=s2):
                            e.matmul(s1[:, 0:nq], lhsT=kT_ap[0:64, :], rhs=qt[0:64, :], start=True, stop=True)
                            return e.matmul(s2[:, 0:nq], lhsT=kT_ap[64:128, :], rhs=qt[64:128, :], start=True, stop=True)
                        S.op('pe', kbufs, [s1_b, s2_b], qk)
                        PP.tick()
                        p, p_b = pTr.next()
                        S.op('act', [s1_b, s2_b], [p_b], lambda e, p=p, pair=pair: e.activation(
                            out=p[:, 0:2 * nq].rearrange("p (a b) -> p a b", a=2), in_=pair[:, :, 0:nq], func=AF.Exp, scale=scale))
                        PP.push(lambda bi=bi, v_ap=v_ap, p=p, p_b=p_b: pv_stage(bi, v_ap, p, p_b))

                    def epi_a():
                        rd, rd_b = rdr.next(); f1, f1_b = f1r.next(); f2, f2_b = f2r.next()
                        S.op('dve', [oa_b], [rd_b], lambda e: e.reciprocal(out=rd[:, 0:nq], in_=oa[:, 256:256 + nq]))
                        S.op('dve', [oa_b, rd_b], [f1_b], lambda e: e.tensor_tensor(out=f1[:, 0:nq], in0=oa[:, 0:nq], in1=rd[:, 0:nq], op=ALU.mult))
                        S.op('dve', [ob_b, rd_b], [rd_b], lambda e: e.reciprocal(out=rd[:, 0:nq], in_=ob[:, 256:256 + nq]))
                        S.op('dve', [ob_b, rd_b], [f2_b], lambda e: e.tensor_tensor(out=f2[:, 0:nq], in0=ob[:, 0:nq], in1=rd[:, 0:nq], op=ALU.mult))
                        S.op('dve', [f1_b, f2_b, lsc_b], [f1_b], lambda e: e.scalar_tensor_tensor(out=f1[:, 0:nq], in0=f2[:, 0:nq], scalar=neglam, in1=f1[:, 0:nq],
                                                                                                     op0=ALU.mult, op1=ALU.add))
                        S.op('dve', [f1_b], [f2_b], lambda e: e.tensor_tensor(out=f2[:, 0:nq], in0=f1[:, 0:nq], in1=f1[:, 0:nq], op=ALU.mult))

                        def epi_b():
                            sq, sq_b = pS.next()
                            S.op('pe', [f2_b, ones_f_b], [sq_b], lambda e: e.matmul(sq[:, 0:nq], lhsT=ones_f[:], rhs=f2[:, 0:nq], start=True, stop=True))
                            if 'sqrt' in _osq.environ.get('KDBG', ''):
                                S.op('act', [sq_b, eps_rms_b], [rd_b], lambda e: e.activation(out=rd[:, 0:nq], in_=sq[:, 0:nq], func=AF.Sqrt, scale=1.0 / 128, bias=eps_rms[:]))
                                S.op('dve', [rd_b], [rd_b], lambda e: e.reciprocal(out=rd[:, 0:nq], in_=rd[:, 0:nq]))
                            else:
                                S.op('act', [sq_b, eps_rms_b], [rd_b], lambda e: e.activation(out=rd[:, 0:nq], in_=sq[:, 0:nq], func=AF.Ln, scale=1.0 / 128, bias=eps_rms[:]))
                                S.op('act', [rd_b], [rd_b], lambda e: e.activation(out=rd[:, 0:nq], in_=rd[:, 0:nq], func=AF.Exp, scale=-0.5))
                            S.op('dve', [f1_b, rd_b, gainc_b], [out_b], lambda e: e.scalar_tensor_tensor(out=out_ap, in0=f1[:, 0:nq], scalar=gainc[:, 0:1], in1=rd[:, 0:nq],
                                                                                                           op0=ALU.mult, op1=ALU.mult))
                            if after is not None:
                                after()
                        PP.push_late(10, epi_b)
                    PP.push(epi_a)

                sc128 = 128 ** -0.5
                sc64 = 64 ** -0.5
                nsb = c.NB
                ktr = Ring([S.sb(es, 'kt', [128, S_], BF16) for _ in range(2)])
                vtr = Ring([S.sb(es, 'vt', [128, nsb, 128], BF16) for _ in range(2)])
                qtr = Ring([S.sb(es, 'qt', [128, S_], BF16) for _ in range(2)])
                osr = Ring([S.sb(es, 'os', [128, 512], BF16) for _ in range(3)])
                opr = Ring([S.sb(es, 'op', [128, 512], BF16) for _ in range(2)])

                def store_o(os_, os_b, chunk, tt):
                    S.dma(ATQ, out=O_d[tt, :, chunk, :], in_=os_[:], reads=[os_b])

                Et, Et_b = S.sb(es, 'Et', [128, NV, 128], BF16)
                Ef, Ef_b = S.sb(es, 'Ef', [128, NV, 128], F32)
                for h in (range(c.HA) if 'A' in KMIX else []):
                    kt, kt_b = ktr.next(); vt, vt_b = vtr.next(); qt, qt_b = qtr.next()
                    load_kv(kt, kt_b, vt, vt_b, c.CKA + h, h * 128, 0, nsb)
                    S.dma('sp', out=qt[:], in_=QK_d[c.CQA + h, :, 0:S_], writes=[qt_b])
                    S.dma('sp', out=Ef[:], in_=rpbt[l, h].rearrange("v k q -> k v q"), writes=[Ef_b])
                    S.op('act', [Ef_b], [Et_b], lambda e: e.activation(out=Et[:], in_=Ef[:], func=AF.Exp))
                    kb_ = [kt_b, vt_b, qt_b, ckT_b, cvt_b]
                    ctx_blocks = [(ckT[:, h, j * 128:(j + 1) * 128], cvt[:, j, h * 128:(h + 1) * 128]) for j in range(c.PB)]
                    for i in range(nsb):
                        if i % 4 == 0:
                            os_, os_b = osr.next()
                        ms, v0 = blocks_na[i]
                        groups = [(ctx_blocks, None)]
                        for g0 in range(0, len(ms), 4):
                            mm_ = ms[g0:g0 + 4]
                            blks = [(kt[:, m * 128:(m + 1) * 128], vt[:, m, :]) for m in mm_]
                            ev = Et[:, v0 + g0:v0 + g0 + len(mm_), :]
                            groups.append((blks, (((lambda a: a.rearrange("p (a b) -> p a b", b=128)), ev), Et_b)))
                        aft = (lambda os_=os_, os_b=os_b, h=h, tt=i // 4: store_o(os_, os_b, h, tt)) if i % 4 == 3 else None
                        attn_single(groups, 128, qt[:, i * 128:(i + 1) * 128], sc128, None, os_[:, (i % 4) * 128:(i % 4 + 1) * 128], os_b, kb_, after=aft)
                    kp, kp_b = ktr.next(); vp, vp_b = vtr.next(); op_, op_b = opr.next()
                    load_kv(kp, kp_b, vp, vp_b, c.CKA + h, h * 128, S_, 4)
                    S.dma('sp', out=kp[:, 512:1024], in_=QK_d[c.CQA + h, :, S_:S_ + 512], writes=[kp_b])
                    for b in range(c.NP):
                        blks = [(kp[:, (2 * b + j) * 128:(2 * b + j + 1) * 128], vp[:, 2 * b + j, :]) for j in range(2)]
                        aft = (lambda op_=op_, op_b=op_b, h=h: S.dma(ATQ, out=O_d[NTS, :, h, :], in_=op_[:], reads=[op_b])) if b == c.NP - 1 else None
                        attn_single([(blks, None)], 256, kp[:, 512 + b * 256:512 + (b + 1) * 256], sc128, None,
                                    op_[:, b * 256:(b + 1) * 256], op_b, [kp_b, vp_b], after=aft)
                    PP.flush()

                for h in (range(c.HB) if 'B' in KMIX else []):
                    kt, kt_b = ktr.next(); vt, vt_b = vtr.next(); qt, qt_b = qtr.next()
                    load_kv(kt, kt_b, vt, vt_b, c.CKB + h, c.MIXA + h * 128, 0, nsb)
                    S.dma('sp', out=qt[:], in_=QK_d[c.CQB + h, :, 0:S_], writes=[qt_b])
                    kb_ = [kt_b, vt_b, qt_b, ckT_b, cvt_b]
                    blks = [(kt[:, m * 128:(m + 1) * 128], vt[:, m, :]) for m in range(nsb)]
                    blks += [(ckT[:, c.HA + h, j * 128:(j + 1) * 128], cvt[:, j, c.MIXA + h * 128:c.MIXA + (h + 1) * 128]) for j in range(c.PB)]
                    for qi in range(S_ // 256):
                        if qi % 2 == 0:
                            os_, os_b = osr.next()
                        aft = (lambda os_=os_, os_b=os_b, h=h, tt=qi // 2: store_o(os_, os_b, c.HA + h, tt)) if qi % 2 == 1 else None
                        attn_diff(blks, 256, qt[:, qi * 256:(qi + 1) * 256], sc64, os_[:, (qi % 2) * 256:(qi % 2 + 1) * 256], os_b, kb_, after=aft)
                    kp, kp_b = ktr.next(); vp, vp_b = vtr.next(); op_, op_b = opr.next()
                    load_kv(kp, kp_b, vp, vp_b, c.CKB + h, c.MIXA + h * 128, S_, 4)
                    S.dma('sp', out=kp[:, 512:1024], in_=QK_d[c.CQB + h, :, S_:S_ + 512], writes=[kp_b])
                    for b in range(c.NP):
                        blks = [(kp[:, (2 * b + j) * 128:(2 * b + j + 1) * 128], vp[:, 2 * b + j, :]) for j in range(2)]
                        aft = (lambda op_=op_, op_b=op_b, h=h: S.dma(ATQ, out=O_d[NTS, :, c.HA + h, :], in_=op_[:], reads=[op_b])) if b == c.NP - 1 else None
                        attn_diff(blks, 256, kp[:, 512 + b * 256:512 + (b + 1) * 256], sc64, op_[:, b * 256:(b + 1) * 256], op_b, [kp_b, vp_b], after=aft)
                    PP.flush()

                qcr = Ring([S.sb(es, 'qc', [128, c.TB, 4, 128], BF16) for _ in range(1)])
                ocr = Ring([S.sb(es, 'oc', [128, 4, 512], BF16) for _ in range(2)])
                b3 = lambda a: a.rearrange("p (g q) -> p g q", g=4)
                for n in (range(c.KVC) if 'C' in KMIX else []):
                    kt, kt_b = ktr.next(); vt, vt_b = vtr.next()
                    qc_, qc_b = qcr.next()
                    load_kv(kt, kt_b, vt, vt_b, c.CKC + n, c.MIXA + c.MIXB + n * 128, 0, nsb)
                    S.dma('sp', out=qc_[:], in_=QC_d[n], writes=[qc_b])
                    kp, kp_b = ktr.next(); vp, vp_b = vtr.next()
                    load_kv(kp, kp_b, vp, vp_b, c.CKC + n, c.MIXA + c.MIXB + n * 128, S_, 4)
                    kb_ = [kt_b, vt_b, qc_b, ckT_b, cvt_b, kp_b, vp_b]
                    hh = c.HA + c.HB + n
                    ctx_blocks = [(ckT[:, hh, j * 128:(j + 1) * 128], cvt[:, j, c.MIXA + c.MIXB + n * 128:c.MIXA + c.MIXB + (n + 1) * 128]) for j in range(c.PB)]
                    es_ap = esink[:, n * 4:(n + 1) * 4].unsqueeze(2).to_broadcast([128, 4, 128])
                    for i in range(c.TB):
                        if i % 4 == 0:
                            oc, oc_b = ocr.next()
                        groups = []
                        if i < nsb:
                            for m in (i - 1, i, i + 1):
                                if m < 0 or m >= nsb:
                                    continue
                                post = None
                                if m != i:
                                    mk = trimt[:, 0 if m < i else 1, :].unsqueeze(1).to_broadcast([128, 4, 128])
                                    post = ((b3, mk), trimt_b)
                                groups.append(([(kt[:, m * 128:(m + 1) * 128], vt[:, m, :])], post))
                            for blk in ctx_blocks:
                                groups.append(([blk], None))
                        else:
                            b = (i - nsb) // 2
                            for j in range(2):
                                groups.append(([(kp[:, (2 * b + j) * 128:(2 * b + j + 1) * 128], vp[:, 2 * b + j, :])], None))
                        qap = qc_[:, i, :, :].rearrange("p g q -> p (g q)")
                        ch0 = c.HA + c.HB + n * 4
                        aft = (lambda oc=oc, oc_b=oc_b, ch0=ch0, tt=i // 4: S.dma(ATQ, out=O_d[tt, :, ch0:ch0 + 4, :], in_=oc[:], reads=[oc_b])) if i % 4 == 3 else None
                        attn_single(groups, 512, qap, sc128, (es_ap, esink_b, b3), oc[:, :, (i % 4) * 128:(i % 4 + 1) * 128], oc_b, kb_, oshp=b3, after=aft)
                    PP.flush()
                S.flush()

            with ExitStack() as es:
                slabs = [dict(fs=fs, segs=[(fs * 512, 512, 0)]) for fs in range(D // 512)]
                pA = Ring([S.ps(es, 'pA', [128, 512]) for _ in range(2)])
                pB = Ring([S.ps(es, 'pB', [128, 512]) for _ in range(2)])
                pC = Ring([S.ps(es, 'pC', [128, 512]) for _ in range(2)])
                gtr = Ring([S.sb(es, 'gt', [128, 3, 4, 512], BF16) for _ in range(2)])
                mstr = Ring([S.sb(es, 'mst', [128, 4, 512], BF16) for _ in range(2)])
                t1r = Ring([S.sb(es, 't1', [128, 512], F32) for _ in range(2)])
                t2r = Ring([S.sb(es, 't2', [128, 512], F32) for _ in range(2)])
                ka = list(range(0, c.HA)); kb2 = list(range(c.HA, c.HA + c.HB)); kc2 = list(range(c.HA + c.HB, KC))

                def br_compute(slab, tt, wb, wb_b, ab, ab_b):
                    fs = slab['fs']
                    gt, gt_b = gtr.next(); mst, mst_b = mstr.next()
                    S.dma('sp', out=gt[:], in_=G_d[tt, fs], writes=[gt_b])
                    for ci in range(4):
                        a_, a_b = pA.next(); b_, b_b = pB.next(); c_, c_b = pC.next()
                        fm_group(a_, a_b, wb, wb_b, ab, ab_b, KC, ci, ks=ka)
                        fm_group(b_, b_b, wb, wb_b, ab, ab_b, KC, ci, ks=kb2)
                        fm_group(c_, c_b, wb, wb_b, ab, ab_b, KC, ci, ks=kc2)
                        t1, t1_b = t1r.next(); t2, t2_b = t2r.next()
                        S.op('dve', [a_b, gt_b], [t1_b], lambda e, t1=t1, a_=a_, gt=gt, ci=ci: e.tensor_tensor(out=t1[:], in0=a_[:], in1=gt[:, 0, ci, :], op=ALU.mult))
                        S.op('dve', [b_b, gt_b], [t2_b], lambda e, t2=t2, b_=b_, gt=gt, ci=ci: e.tensor_tensor(out=t2[:], in0=b_[:], in1=gt[:, 1, ci, :], op=ALU.mult))
                        S.op('dve', [t1_b, t2_b], [t1_b], lambda e, t1=t1, t2=t2: e.tensor_tensor(out=t1[:], in0=t1[:], in1=t2[:], op=ALU.add))
                        S.op('dve', [c_b, gt_b, t2_b], [t2_b], lambda e, t2=t2, c_=c_, gt=gt, ci=ci: e.tensor_tensor(out=t2[:], in0=c_[:], in1=gt[:, 2, ci, :], op=ALU.mult))
                        S.op('dve', [t1_b, t2_b], [mst_b], lambda e, t1=t1, t2=t2, mst=mst, ci=ci: e.tensor_tensor(out=mst[:, ci, :], in0=t1[:], in1=t2[:], op=ALU.add))
                    S.dma('pool', out=M_d[tt, :, fs * 4:(fs + 1) * 4, :], in_=mst[:], reads=[mst_b])

                run_gemm(es, w_branch[l], slabs, O_d, KC, 512, NT, 512, br_compute)
                S.flush()

            def tm_gemm_phase(W, act_d, KCg, TT, NTT, NW):
                with ExitStack() as es:
                    slabs = [dict(n0=n0, segs=[(n0, NW, 0)]) for n0 in range(0, D, NW)]
                    pr = Ring([S.ps(es, 'py', [128, 512]) for _ in range(6)])
                    nsub = TT // 128
                    ystr = Ring([S.sb(es, 'yst', [128, nsub, NW], F32) for _ in range(3)])

                    def comp(slab, tt, wb, wb_b, ab, ab_b):
                        yst, yst_b = ystr.next()
                        for sub in range(nsub):
                            pb, pb_b = pr.next()
                            tm_group(pb, pb_b, wb, wb_b, ab, ab_b, KCg, sub, NW)
                            copy_evac(yst[:, sub, :], pb[:, 0:NW], [pb_b], [yst_b])
                        S.dma('sp', out=y_d[tt * TT:(tt + 1) * TT, slab['n0']:slab['n0'] + NW].rearrange("(s p) n -> p s n", p=128),
                              in_=yst[:], reads=[yst_b])
                    run_gemm(es, W, slabs, act_d, KCg, TT, NTT, NW, comp, kgroup=(4 if NW == 512 else 8))
                    S.flush()

            def ln_phase(xsrc, xdst, li, nxt):
                with ExitStack() as es:
                    ht = HT(es, LNQ, True) if nxt is not None else None
                    xr = Ring([S.sb(es, 'lx', [128, D], F32) for _ in range(2)])
                    yr = Ring([S.sb(es, 'ly', [128, D], F32) for _ in range(1)])
                    zr = Ring([S.sb(es, 'lz', [128, D], F32) for _ in range(2)])
                    gbc, gbc_b = S.sb(es, 'gbc', [128, D], F32)
                    lg, lg_b = S.sb(es, 'lg', [128, D], F32)
                    lb, lb_b = S.sb(es, 'lb', [128, D], F32)
                    S.dma('sp', out=lg[:], in_=ln_g[l, li:li + 1, :].to_broadcast([128, D]), writes=[lg_b])
                    S.dma('sp', out=lb[:], in_=ln_b[l, li:li + 1, :].to_broadcast([128, D]), writes=[lb_b])
                    nst = D // 512 if D >= 512 else 1
                    fw = D // nst
                    str_ = Ring([S.sb(es, 'st', [128, nst, 6], F32) for _ in range(2)])
                    mvr = Ring([S.sb(es, 'mv', [128, 4], F32) for _ in range(2)])
                    gi = 3 * li + 2
                    def stage_a(tb):
                        g = 0 if tb < S_ // 128 else 1
                        if tb == 0 or tb == S_ // 128:
                            S.dma('sp', out=gbc[:], in_=mod_d[l, g:g + 1, gi * D:(gi + 1) * D].to_broadcast([128, D]), writes=[gbc_b])
                        xt, xt_b = xr.next(); yt, yt_b = yr.next(); zt, zt_b = zr.next()
                        st, st_b = str_.next(); mv, mv_b = mvr.next()
                        S.dma('sp', out=xt[:], in_=xrows(xsrc, tb), writes=[xt_b])
                        S.dma('sp', out=yt[:], in_=y_d[tb * 128:(tb + 1) * 128, :], writes=[yt_b])
                        S.op('dve', [yt_b, gbc_b], [yt_b], lambda e, yt=yt: e.tensor_tensor(out=yt[:], in0=yt[:], in1=gbc[:], op=ALU.mult))
                        S.op('dve', [xt_b, yt_b], [xt_b], lambda e, xt=xt, yt=yt: e.scalar_tensor_tensor(out=xt[:], in0=xt[:], scalar=c.ALPHA, in1=yt[:], op0=ALU.mult, op1=ALU.add))

                        def stats(e, xt=xt, st=st):
                            for j in range(nst):
                                ins = e.bn_stats(out=st[:, j, :], in_=xt[:, j * fw:(j + 1) * fw])
                            return ins
                        S.op('dve', [xt_b], [st_b], stats)
                        S.op('dve', [st_b], [mv_b], lambda e, mv=mv, st=st: e.bn_aggr(out=mv[:, 0:2], in_=st[:].rearrange("p a b -> p (a b)")))
                        S.op('act', [mv_b, eps_ln_b], [mv_b], lambda e, mv=mv: e.activation(out=mv[:, 2:3], in_=mv[:, 1:2], func=AF.Sqrt, bias=eps_ln[:], scale=1.0))
                        S.op('dve', [mv_b], [mv_b], lambda e, mv=mv: e.reciprocal(out=mv[:, 2:3], in_=mv[:, 2:3]))
                        S.op('dve', [mv_b], [mv_b], lambda e, mv=mv: e.tensor_scalar(out=mv[:, 3:4], in0=mv[:, 0:1], scalar1=-1.0, scalar2=mv[:, 2:3], op0=ALU.mult, op1=ALU.mult))
                        S.op('act', [xt_b, mv_b], [zt_b], lambda e, zt=zt, xt=xt, mv=mv: e.activation(out=zt[:], in_=xt[:], func=AF.Identity, scale=mv[:, 2:3], bias=mv[:, 3:4]))
                        return (tb, zt, zt_b)

                    def stage_b(ctx):
                        tb, zt, zt_b = ctx
                        S.op('dve', [zt_b, lg_b], [zt_b], lambda e, zt=zt: e.tensor_tensor(out=zt[:], in0=zt[:], in1=lg[:], op=ALU.mult))
                        S.op('dve', [zt_b, lb_b], [zt_b], lambda e, zt=zt: e.tensor_tensor(out=zt[:], in0=zt[:], in1=lb[:], op=ALU.add))
                        S.dma(LNQ, out=xrows(xdst, tb), in_=zt[:], reads=[zt_b])
                        if ht is not None:
                            ht.emit(zt, zt_b, tb, nxt[0], nxt[1])

                    prev = None
                    for tb in range(c.TB):
                        cur = stage_a(tb)
                        if prev is not None:
                            stage_b(prev)
                        prev = cur
                    stage_b(prev)
                    S.flush()

            tm_gemm_phase(w_out[l], M_d, KC, 512, NT, 512)
            ln_phase('in' if l == 0 else x2_d, x1_d, 0, (l, 1))

            with ExitStack() as es:
                slabs = [dict(j=j, segs=[(j * 256, 256, 0), (c.DFF + j * 256, 256, 256)]) for j in range(FC // 2)]
                pr = Ring([S.ps(es, 'pg', [128, 512]) for _ in range(8)])
                astr = Ring([S.sb(es, 'ast', [128, 2, 512], BF16) for _ in range(3)])
                sgr = Ring([S.sb(es, 'sg', [128, 512], F32) for _ in range(3)])

                def gu_compute(slab, tt, wb, wb_b, ab, ab_b):
                    j = slab['j']
                    ast, ast_b = astr.next()
                    pbs = []
                    for ci in range(4):
                        pb, pb_b = pr.next()
                        fm_group(pb, pb_b, wb, wb_b, ab, ab_b, KC, ci)
                        pbs.append((pb, pb_b))
                    for cc in range(2):
                        (pg, pg_b), (pu, pu_b) = pbs[cc], pbs[2 + cc]
                        sg, sg_b = sgr.next()
                        S.op('act', [pg_b], [sg_b], lambda e, sg=sg, pg=pg: e.activation(out=sg[:], in_=pg[:], func=AF.Silu))
                        S.op('dve', [sg_b, pu_b], [ast_b], lambda e, sg=sg, pu=pu, ast=ast, cc=cc: e.tensor_tensor(out=ast[:, cc, :], in0=pu[:], in1=sg[:], op=ALU.mult))
                    for hh in range(2):
                        S.dma('sp', out=A_d[tt * 2 + hh, :, j * 2:j * 2 + 2, :], in_=ast[:, :, hh * 256:(hh + 1) * 256], reads=[ast_b])

                run_gemm(es, w_gu[l], slabs, hT_d, KC, 512, NT, 512, gu_compute)
                S.flush()

            tm_gemm_phase(w_down[l], A_d, FC, 256, 2 * NT, 256)
            last = (l == DEPTH - 1)
            ln_phase(x1_d, 'out' if last else x2_d, 1, None if last else (l + 1, 0))

        S.flush(final=True)
    return nc, NV, blocks_na, var_tiles


_CACHE = {}


def host_constants(cfg, var_tiles, rpb_a):
    c = cfg
    NV = len(var_tiles)
    dr = np.stack([t[0] for t in var_tiles]); dc = np.stack([t[1] for t in var_tiles]); ok = np.stack([t[2] for t in var_tiles])
    g = rpb_a[:, :, dr, dc]
    rpbt = np.where(ok[None, None], g, np.float32(NEG)).astype(np.float32)
    W = max(c.S, 128)
    ropeB = np.zeros((3, 128, W), np.float32); ropeC = np.zeros((3, 128, W), np.float32)
    cb, sb_, pb = rope_tables(c.S, 'B'); cc, sc, pc = rope_tables(c.S, 'C')
    ropeB[0, :, :c.S] = cb; ropeB[1, :, :c.S] = sb_; ropeB[2, :, :128] = pb
    ropeC[0, :, :c.S] = cc; ropeC[1, :, :c.S] = sc; ropeC[2, :, :128] = pc
    k = np.arange(128)[:, None]; q = np.arange(128)[None, :]
    trim = np.stack([(k >= q), (k <= q)]).astype(np.float32)
    return dict(rpbt=rpbt, ropeB=ropeB, ropeC=ropeC, trim=trim, ident=np.eye(128, dtype=np.float32))


def run(cfg, inputs, n_cores):
    c = cfg
    key = (c.D, c.S, n_cores)
    if key not in _CACHE:
        _CACHE[key] = build(cfg)
    nc, NV, blocks_na, var_tiles = _CACHE[key]
    f = lambda a: np.ascontiguousarray(np.asarray(a, dtype=np.float32))
    consts = host_constants(cfg, var_tiles, f(inputs['rpb_a']))
    shared = dict(w_ada=f(inputs['w_ada']), b_ada=f(inputs['b_ada']), w_in=f(inputs['w_in']), w_branch=f(inputs['w_branch']),
                  w_out=f(inputs['w_out']), w_gate_up=f(inputs['w_gate_up']), w_down=f(inputs['w_down']),
                  ln_g=f(inputs['ln_g']), ln_b=f(inputs['ln_b']), lam_b=f(inputs['lam_b']).reshape(c.DEPTH, 256),
                  subln_b=f(inputs['subln_b']), sink_c=f(inputs['sink_c']), **consts)
    xs = f(inputs['x_sample']); xp = f(inputs['x_prompt'])
    cc_ = f(inputs['c']); cctx = f(inputs['c_ctx'])
    in_maps = []
    for i in range(n_cores):
        m = dict(shared)
        m['xs'] = xs[i]
        m['xp'] = xp[c.NP * i:c.NP * (i + 1)].reshape(c.NP * c.SEQ, c.D)
        m['cak'] = f(inputs['cache_a_k'])[i].reshape(c.DEPTH, c.PAST, c.MIXA)
        m['cav'] = f(inputs['cache_a_v'])[i].reshape(c.DEPTH, c.PAST, c.MIXA)
        m['cbk'] = f(inputs['cache_b_k'])[i].reshape(c.DEPTH, c.PAST, c.MIXB)
        m['cbv'] = f(inputs['cache_b_v'])[i].reshape(c.DEPTH, c.PAST, c.MIXB)
        m['cck'] = f(inputs['cache_c_k'])[i].reshape(c.DEPTH, c.PAST, c.KVW)
        m['ccv'] = f(inputs['cache_c_v'])[i].reshape(c.DEPTH, c.PAST, c.KVW)
        m['cvec'] = np.ascontiguousarray(np.stack([cc_[i], cctx]))
        in_maps.append(m)
    res = run_bass_kernel_spmd(nc, in_maps, core_ids=list(range(n_cores)))
    R = res.results
    cat = lambda k: np.concatenate([np.asarray(r[k]) for r in R], axis=0)
    y_prompt = cat('yp').reshape(n_cores * c.NP, c.SEQ, c.D)
    y_sample = np.stack([np.asarray(r['ys']) for r in R])
    nB = n_cores * c.NP
    outs = (y_prompt, y_sample,
            cat('nak').reshape(nB, c.DEPTH, c.SEQ, c.HA, 128), cat('nav').reshape(nB, c.DEPTH, c.SEQ, c.HA, 128),
            cat('nbk').reshape(nB, c.DEPTH, c.SEQ, c.HB, 2, 64), cat('nbv').reshape(nB, c.DEPTH, c.SEQ, c.HB, 128),
            cat('nck').reshape(nB, c.DEPTH, c.SEQ, c.KVC, 128), cat('ncv').reshape(nB, c.DEPTH, c.SEQ, c.KVC, 128))
    return tuple(np.ascontiguousarray(o, dtype=np.float32) for o in outs)


def kernel(**inputs):
    return run(Cfg(), inputs, 8)
```
